# Optimizing a Trainium2 kernel written in Bass

```python
import math
import jax
import jax.numpy as jnp
from jax import lax
import numpy as np

D_MODEL = 2048
BATCH = 16
SEQ = 2048
DEPTH = 2
DEC_BATCH = 32
DEC_SEQ = 64
PAST_LEN = 4096

CHUNK = 64
EPS = 1e-6

W_A = 1024
HEAD_K_A = 128
N_HEADS_A = W_A // HEAD_K_A
HEAD_V_A = W_A // N_HEADS_A

N_Q_B = 16
N_KV_B = 4
HEAD_DIM_B = 64
GQA_GROUP = N_Q_B // N_KV_B
W_B = N_Q_B * HEAD_DIM_B
KV_W_B = N_KV_B * HEAD_DIM_B
WINDOW = 128
ROT_DIM = HEAD_DIM_B // 4
ROPE_THETA = 500000.0
ATTN_SCALE = HEAD_DIM_B ** -0.5

W_C = 1024
HEAD_DIM_C = 64
N_HEADS_C = W_C // HEAD_DIM_C
N_GROUPS_C = 4
D_STATE = 128
CONV_W = 4
CONV_DIM = W_C + 2 * N_GROUPS_C * D_STATE

IN_SPLITS = (W_A, W_A, W_A, W_A, W_B, KV_W_B, KV_W_B, W_B, W_C, CONV_DIM, N_HEADS_C, D_MODEL, D_MODEL, D_MODEL)
D_IN = sum(IN_SPLITS)

kernel_name = 'hybrid_stream_hgrn2_swa_ssd_step'


def rmsnorm(x, w):
    xf = x.astype(jnp.float32)
    r = lax.rsqrt(jnp.mean(xf * xf, axis=-1, keepdims=True) + EPS)
    return (xf * r).astype(x.dtype) * w


def split_columns(t):
    idx = [int(i) for i in np.cumsum(IN_SPLITS)[:-1]]
    return jnp.split(t, idx, axis=-1)


def partial_rotary(x, pos):
    half = ROT_DIM // 2
    inv = jnp.power(ROPE_THETA, -jnp.arange(half, dtype=jnp.float32) / half)
    ang = pos.astype(jnp.float32)[:, None] * inv[None, :]
    cos = jnp.cos(ang)[None, :, None, :]
    sin = jnp.sin(ang)[None, :, None, :]
    xr = x[..., :ROT_DIM].astype(jnp.float32)
    x1, x2 = xr[..., :half], xr[..., half:]
    rot = jnp.concatenate([x1 * cos - x2 * sin, x2 * cos + x1 * sin], axis=-1).astype(x.dtype)
    return jnp.concatenate([rot, x[..., ROT_DIM:]], axis=-1)


def sink_softmax(s, sink):
    sink = sink.astype(jnp.float32)
    m = jnp.maximum(jnp.max(s, axis=-1, keepdims=True), sink)
    p = jnp.exp(s - m)
    return p / (jnp.sum(p, axis=-1, keepdims=True) + jnp.exp(sink - m))


def swa_banded(q, k, v, sinks):
    bsz, L = q.shape[0], q.shape[1]
    n = L // CHUNK
    nb = WINDOW // CHUNK
    band = WINDOW + CHUNK
    qc = q.reshape(bsz, n, CHUNK, N_KV_B, GQA_GROUP, HEAD_DIM_B)
    pad = jnp.zeros((bsz, WINDOW, N_KV_B, HEAD_DIM_B), k.dtype)
    kp = jnp.concatenate([pad, k], axis=1)
    vp = jnp.concatenate([pad, v], axis=1)

    def band_view(t):
        return jnp.concatenate(
            [t[:, j * CHUNK: j * CHUNK + L].reshape(bsz, n, CHUNK, N_KV_B, HEAD_DIM_B) for j in range(nb + 1)],
            axis=2)

    kb, vb = band_view(kp), band_view(vp)
    key_pos = jnp.arange(n)[:, None] * CHUNK - WINDOW + jnp.arange(band)[None, :]
    valid = key_pos >= 0
    s = jnp.einsum('bnqhgd,bnkhd->bnhgqk', qc, kb).astype(jnp.float32) * ATTN_SCALE
    s = jnp.where(valid[None, :, None, None, None, :], s, -jnp.inf)
    p = sink_softmax(s, sinks.reshape(N_KV_B, GQA_GROUP)[:, :, None, None])
    o = jnp.einsum('bnhgqk,bnkhd->bnqhgd', p.astype(v.dtype), vb)
    return o.reshape(bsz, L, W_B)


def swa_cached(q, k_all, v_all, sinks):
    bsz, L = q.shape[0], q.shape[1]
    qg = q.reshape(bsz, L, N_KV_B, GQA_GROUP, HEAD_DIM_B)
    s = jnp.einsum('bqhgd,bkhd->bhgqk', qg, k_all).astype(jnp.float32) * ATTN_SCALE
    p = sink_softmax(s, sinks.reshape(N_KV_B, GQA_GROUP)[:, :, None, None])
    o = jnp.einsum('bhgqk,bkhd->bqhgd', p.astype(v_all.dtype), v_all)
    return o.reshape(bsz, L, W_B)


def hgrn2_scan(q, k, v, log_f, s0):
    bsz, L, H, DK = q.shape
    DV = v.shape[-1]
    C = min(CHUNK, L)
    n = L // C

    def to_chunks(t):
        return jnp.moveaxis(t.reshape(bsz, n, C, H, t.shape[-1]), 1, 0)

    causal = jnp.tril(jnp.ones((C, C), dtype=bool))

    def step(S, inp):
        qc, kc, vc, gc = inp
        b = jnp.cumsum(gc, axis=1)
        pair = jnp.where(causal[None, :, :, None, None], b[:, :, None] - b[:, None, :], -jnp.inf)
        A = jnp.einsum('bthk,bshk,btshk->bhts', qc, kc, jnp.exp(pair))
        o = jnp.einsum('bhts,bshv->bthv', A, vc) + jnp.einsum('bthk,bhkv->bthv', qc * jnp.exp(b), S)
        b_last = b[:, -1]
        S = jnp.exp(b_last)[..., None] * S + jnp.einsum('bshk,bshv->bhkv', kc * jnp.exp(b_last[:, None] - b), vc)
        return S, o

    S, o = lax.scan(step, s0, (to_chunks(q), to_chunks(k), to_chunks(v), to_chunks(log_f)))
    return jnp.moveaxis(o, 0, 1).reshape(bsz, L, H, DV), S


def ssd_scan(xdt, log_a, Bm, Cm, h0):
    bsz, L, H, P = xdt.shape
    G, N = Bm.shape[2], Bm.shape[3]
    hg = H // G
    C = min(CHUNK, L)
    n = L // C
    xs = jnp.moveaxis(xdt.reshape(bsz, n, C, G, hg, P), 1, 0)
    gs = jnp.moveaxis(log_a.reshape(bsz, n, C, G, hg), 1, 0)
    bs = jnp.moveaxis(Bm.reshape(bsz, n, C, G, N), 1, 0)
    cs = jnp.moveaxis(Cm.reshape(bsz, n, C, G, N), 1, 0)
    causal = jnp.tril(jnp.ones((C, C), dtype=bool))

    def step(h, inp):
        xc, gc, bc, cc = inp
        b = jnp.cumsum(gc, axis=1)
        seg = jnp.where(causal[None, :, :, None, None], b[:, :, None] - b[:, None, :], -jnp.inf)
        cb = jnp.einsum('btgn,bsgn->btsg', cc, bc)
        y = jnp.einsum('btsg,btsgh,bsghp->btghp', cb, jnp.exp(seg), xc)
        y = y + jnp.einsum('btgn,bghpn->btghp', cc, h) * jnp.exp(b)[..., None]
        bl = b[:, -1]
        h = jnp.exp(bl)[..., None, None] * h + jnp.einsum('bsgn,bsgh,bsghp->bghpn', bc, jnp.exp(bl[:, None] - b), xc)
        return h, y

    h, y = lax.scan(step, h0.reshape(bsz, G, hg, P, N), (xs, gs, bs, cs))
    return jnp.moveaxis(y, 0, 1).reshape(bsz, L, H, P), h.reshape(bsz, H, P, N)


def causal_conv(u, prev, w, b):
    L = u.shape[1]
    up = jnp.concatenate([prev.astype(u.dtype), u], axis=1)
    acc = b
    for j in range(CONV_W):
        acc = acc + up[:, j:j + L] * w[j]
    return jax.nn.silu(acc), up[:, -(CONV_W - 1):]


def trunk_layer(x, pos, k_cache, v_cache, s_hgrn, s_ssm, s_conv, lb,
                norm_pre, norm_post, w_in, hgrn_norm, swa_sinks, conv_w, conv_b,
                dt_bias, a_log, d_skip, ssm_norm, w_branch_a, w_branch_b, w_branch_c, w_out):
    f32 = jnp.float32
    bsz, L, _ = x.shape
    prompt = k_cache is None
    if prompt:
        s_hgrn = jnp.zeros((bsz, N_HEADS_A, HEAD_K_A, HEAD_V_A), f32)
        s_ssm = jnp.zeros((bsz, N_HEADS_C, HEAD_DIM_C, D_STATE), f32)
        s_conv = jnp.zeros((bsz, CONV_W - 1, CONV_DIM), x.dtype)

    h = rmsnorm(x, norm_pre)
    proj = jnp.einsum('bld,de->ble', h, w_in)
    (a_q, a_f, a_i, a_g, b_q, b_k, b_v, b_g, c_z, c_xbc, c_dt, g_a, g_b, g_c) = split_columns(proj)

    za = a_f.astype(f32).reshape(bsz, L, N_HEADS_A, HEAD_K_A)
    lbh = lb.reshape(N_HEADS_A, HEAD_K_A)
    log_f = jnp.logaddexp(jnp.log(lbh), jnp.log1p(-lbh) + jax.nn.log_sigmoid(za))
    k_a = (1.0 - lbh) * jax.nn.sigmoid(-za)
    q_a = jax.nn.silu(a_q.astype(f32)).reshape(bsz, L, N_HEADS_A, HEAD_K_A)
    v_a = a_i.astype(f32).reshape(bsz, L, N_HEADS_A, HEAD_V_A)
    o_a, hgrn_new = hgrn2_scan(q_a, k_a, v_a, log_f, s_hgrn.astype(f32))
    y_a = (rmsnorm(o_a, hgrn_norm).reshape(bsz, L, W_A) * jax.nn.silu(a_g.astype(f32))).astype(x.dtype)

    q_b = partial_rotary(b_q.reshape(bsz, L, N_Q_B, HEAD_DIM_B), pos)
    k_b = partial_rotary(b_k.reshape(bsz, L, N_KV_B, HEAD_DIM_B), pos)
    v_b = b_v.reshape(bsz, L, N_KV_B, HEAD_DIM_B)
    if prompt:
        o_b = swa_banded(q_b, k_b, v_b, swa_sinks)
        k_new, v_new = k_b[:, -WINDOW:], v_b[:, -WINDOW:]
    else:
        k_all = jnp.concatenate([k_cache.astype(k_b.dtype), k_b], axis=1)
        v_all = jnp.concatenate([v_cache.astype(v_b.dtype), v_b], axis=1)
        o_b = swa_cached(q_b, k_all, v_all, swa_sinks)
        k_new, v_new = k_all[:, -WINDOW:], v_all[:, -WINDOW:]
    y_b = o_b * jax.nn.silu(b_g)

    xbc, conv_new = causal_conv(c_xbc, s_conv, conv_w, conv_b)
    x_c, b_c, c_c = jnp.split(xbc, [W_C, W_C + N_GROUPS_C * D_STATE], axis=-1)
    dt = jax.nn.softplus(c_dt.astype(f32) + dt_bias.astype(f32))
    log_a = -dt * jnp.exp(a_log.astype(f32))
    x_c = x_c.astype(f32).reshape(bsz, L, N_HEADS_C, HEAD_DIM_C)
    o_c, ssm_new = ssd_scan(x_c * dt[..., None], log_a,
                            b_c.astype(f32).reshape(bsz, L, N_GROUPS_C, D_STATE),
                            c_c.astype(f32).reshape(bsz, L, N_GROUPS_C, D_STATE),
                            s_ssm.astype(f32))
    o_c = (o_c + d_skip.astype(f32)[:, None] * x_c).reshape(bsz, L, W_C) * jax.nn.silu(c_z.astype(f32))
    gsz = W_C // N_GROUPS_C
    y_c = rmsnorm(o_c.reshape(bsz, L, N_GROUPS_C, gsz), ssm_norm.reshape(N_GROUPS_C, gsz)).reshape(bsz, L, W_C).astype(x.dtype)

    merged = (jax.nn.sigmoid(g_a) * (y_a @ w_branch_a)
              + jax.nn.sigmoid(g_b) * (y_b @ w_branch_b)
              + jax.nn.sigmoid(g_c) * (y_c @ w_branch_c))
    out = merged @ w_out
    x = x + rmsnorm(out, norm_post)
    return x, (k_new, v_new, hgrn_new.astype(x.dtype), ssm_new.astype(x.dtype), conv_new)


def setup_inputs(seed: int = 0) -> dict:
    key = jax.random.key(seed)
    ks = jax.random.split(key, 26)
    f32 = jnp.float32

    def nrm(k, shape, s):
        return jax.random.normal(k, shape, f32) * s

    dt0 = jnp.exp(jax.random.uniform(ks[17], (DEPTH, N_HEADS_C), f32, math.log(1e-3), math.log(1e-1)))
    return {
        'x_prompt': nrm(ks[0], (BATCH, SEQ, D_MODEL), 1.0),
        'x_sample': nrm(ks[1], (DEC_BATCH, DEC_SEQ, D_MODEL), 1.0),
        'cache_swa_k': nrm(ks[2], (DEPTH, DEC_BATCH, WINDOW, N_KV_B, HEAD_DIM_B), 1.0),
        'cache_swa_v': nrm(ks[3], (DEPTH, DEC_BATCH, WINDOW, N_KV_B, HEAD_DIM_B), 1.0),
        'state_hgrn': nrm(ks[4], (DEPTH, DEC_BATCH, N_HEADS_A, HEAD_K_A, HEAD_V_A), 0.5),
        'state_ssm': nrm(ks[5], (DEPTH, DEC_BATCH, N_HEADS_C, HEAD_DIM_C, D_STATE), 0.1),
        'state_conv': nrm(ks[6], (DEPTH, DEC_BATCH, CONV_W - 1, CONV_DIM), 1.0),
        'norm_pre': 1.0 + nrm(ks[7], (DEPTH, D_MODEL), 0.02),
        'norm_post': 1.0 + nrm(ks[8], (DEPTH, D_MODEL), 0.02),
        'w_in': nrm(ks[9], (DEPTH, D_MODEL, D_IN), D_MODEL ** -0.5),
        'hgrn_lb_logits': nrm(ks[10], (DEPTH, W_A), 0.5),
        'hgrn_norm': 1.0 + nrm(ks[11], (DEPTH, HEAD_V_A), 0.02),
        'swa_sinks': nrm(ks[12], (DEPTH, N_Q_B), 0.5),
        'conv_w': nrm(ks[13], (DEPTH, CONV_W, CONV_DIM), CONV_W ** -0.5),
        'conv_b': nrm(ks[14], (DEPTH, CONV_DIM), 0.02),
        'dt_bias': dt0 + jnp.log(-jnp.expm1(-dt0)),
        'a_log': jnp.log(jax.random.uniform(ks[15], (DEPTH, N_HEADS_C), f32, 1.0, 16.0)),
        'd_skip': 1.0 + nrm(ks[16], (DEPTH, N_HEADS_C), 0.1),
        'ssm_norm': 1.0 + nrm(ks[18], (DEPTH, W_C), 0.02),
        'w_branch_a': nrm(ks[19], (DEPTH, W_A, D_MODEL), W_A ** -0.5),
        'w_branch_b': nrm(ks[20], (DEPTH, W_B, D_MODEL), W_B ** -0.5),
        'w_branch_c': nrm(ks[21], (DEPTH, W_C, D_MODEL), W_C ** -0.5),
        'w_out': nrm(ks[22], (DEPTH, D_MODEL, D_MODEL), D_MODEL ** -0.5),
    }


def reference(x_prompt, x_sample, cache_swa_k, cache_swa_v, state_hgrn, state_ssm, state_conv,
              norm_pre, norm_post, w_in, hgrn_lb_logits, hgrn_norm, swa_sinks, conv_w, conv_b,
              dt_bias, a_log, d_skip, ssm_norm, w_branch_a, w_branch_b, w_branch_c, w_out):
    lbp = jax.nn.softmax(hgrn_lb_logits.astype(jnp.float32), axis=0)
    lbc = jnp.cumsum(lbp, axis=0)
    lb_all = lbc - lbc[0:1]
    pos_p = jnp.arange(x_prompt.shape[1], dtype=jnp.int32)
    pos_s = PAST_LEN + jnp.arange(x_sample.shape[1], dtype=jnp.int32)
    xp, xs = x_prompt, x_sample
    pst, sst = [], []
    for l in range(DEPTH):
        w = (norm_pre[l], norm_post[l], w_in[l], hgrn_norm[l], swa_sinks[l], conv_w[l], conv_b[l],
             dt_bias[l], a_log[l], d_skip[l], ssm_norm[l], w_branch_a[l], w_branch_b[l], w_branch_c[l], w_out[l])
        xp, sp = trunk_layer(xp, pos_p, None, None, None, None, None, lb_all[l], *w)
        xs, ss = trunk_layer(xs, pos_s, cache_swa_k[l], cache_swa_v[l], state_hgrn[l], state_ssm[l],
                             state_conv[l], lb_all[l], *w)
        pst.append(sp)
        sst.append(ss)
    new_k_prompt = jnp.stack([s[0] for s in pst])
    new_v_prompt = jnp.stack([s[1] for s in pst])
    new_hgrn_prompt = jnp.stack([s[2] for s in pst])
    new_ssm_prompt = jnp.stack([s[3] for s in pst])
    new_conv_prompt = jnp.stack([s[4] for s in pst])
    new_k_sample = jnp.stack([s[0] for s in sst])
    new_v_sample = jnp.stack([s[1] for s in sst])
    new_hgrn_sample = jnp.stack([s[2] for s in sst])
    new_ssm_sample = jnp.stack([s[3] for s in sst])
    new_conv_sample = jnp.stack([s[4] for s in sst])
    return (xp, xs, new_k_prompt, new_v_prompt, new_hgrn_prompt, new_ssm_prompt, new_conv_prompt,
            new_k_sample, new_v_sample, new_hgrn_sample, new_ssm_sample, new_conv_sample)
```

```python
import contextlib
import numpy as np
import concourse.bass as bass
import concourse.mybir as mybir
from concourse.bass_utils import run_bass_kernel_spmd

F32 = mybir.dt.float32
BF16 = mybir.dt.bfloat16
AF = mybir.ActivationFunctionType
ALU = mybir.AluOpType
AX = mybir.AxisListType


class StopBuild(Exception):
    pass


class Sched:
    ENGS = ("pe", "act", "dve", "pool", "sp")
    NDSEM = 6
    NDSEM_POOL = 2

    def __init__(self, nc, stack):
        self.nc = nc
        self.stack = stack
        self.sems = {}
        for e in self.ENGS:
            self.sems[e] = stack.enter_context(nc.semaphore("s_" + e))
        self.dsems = {}
        for e in ("sp", "pool", "act"):
            self.dsems[e] = [stack.enter_context(nc.semaphore(f"d_{e}{i}")) for i in range(self.NDSEM)]
        self.sem_by_id = {}
        self.seq = {e: 0 for e in self.ENGS}
        self.dcount = {e: 0 for e in self.dsems}
        self.dval = {e: [0] * self.NDSEM for e in self.dsems}
        self.waited = {e: {} for e in self.ENGS}
        self.last_w = {}
        self.readers = {}
        self.streams = {e: [] for e in self.ENGS}
        self.nops = 0
        self.oplog = []

    def _semobj(self, name):
        if name in self.sems:
            return self.sems[name]
        q, i = name
        return self.dsems[q][i]

    def _need(self, eng, toks, tok, kind):
        if tok is None:
            return
        semname, value, src = tok
        if src == eng and semname in self.sems:
            if eng == "pe":
                return
        if toks.get(semname, 0) < value:
            toks[semname] = value

    stopn = None

    def op(self, eng, fn, r=(), w=(), dma=False):
        if self.stopn is not None and self.nops >= self.stopn:
            raise StopBuild()
        toks = {}
        r = list(r)
        w = list(w) + [k for k in r if k.startswith("ps")]
        for k in r:
            self._need(eng, toks, self.last_w.get(k), "raw")
        for k in w:
            self._need(eng, toks, self.last_w.get(k), "waw")
            for sname, (val, src) in self.readers.get(k, {}).items():
                self._need(eng, toks, (sname, val, src), "war")
        if dma:
            q = eng
            i = self.dcount[q] % (self.NDSEM_POOL if q == "pool" else self.NDSEM)
            self.dcount[q] += 1
            semname = (q, i)
            prev = self.dval[q][i]
            if prev > 0 and toks.get(semname, 0) < prev:
                toks[semname] = prev
            self.dval[q][i] = prev + 16
            mytok = (semname, prev + 16, q)
            inc = (semname, 16)
        else:
            self.seq[eng] += 1
            mytok = (eng, self.seq[eng], eng)
            inc = (eng, 1)
        waits = []
        wd = self.waited[eng]
        for sname, val in toks.items():
            if wd.get(sname, 0) < val:
                wd[sname] = val
                waits.append((sname, val))
        self.streams[eng].append((waits, fn, inc))
        self.oplog.append((self.nops, eng, list(r), list(w), dma))
        for k in r:
            self.readers.setdefault(k, {})[mytok[0]] = (mytok[1], mytok[2])
        for k in w:
            self.last_w[k] = mytok
            self.readers[k] = {}
        self.nops += 1
        return mytok

    def final_wait(self, eng, toks):
        waits = []
        for (sname, val, _src) in toks:
            waits.append((sname, val))
        self.streams[eng].append((waits, None, None))

    def emit(self):
        nc = self.nc
        engobj = {"pe": "tensor", "act": "scalar", "dve": "vector", "pool": "gpsimd", "sp": "sync"}
        with nc.Block() as block:
            for e in self.ENGS:
                stream = self.streams[e]
                if not stream:
                    continue

                def body(eobj, stream=stream):
                    for waits, fn, inc in stream:
                        for sname, val in waits:
                            eobj.wait_ge(self._semobj(sname), val)
                        if fn is None:
                            continue
                        ins = fn(eobj)
                        ins.then_inc(self._semobj(inc[0]), inc[1])

                getattr(block, engobj[e])(body)

    def barrier(self):
        toks = []
        for e in self.ENGS:
            if self.seq[e] > 0:
                toks.append((e, self.seq[e]))
        for q in self.dsems:
            for i in range(self.NDSEM):
                if self.dval[q][i] > 0:
                    toks.append(((q, i), self.dval[q][i]))
        for e in self.ENGS:
            waits = []
            wd = self.waited[e]
            for sname, val in toks:
                if sname == e and e == "pe":
                    continue
                if wd.get(sname, 0) < val:
                    wd[sname] = val
                    waits.append((sname, val))
            if waits:
                self.streams[e].append((waits, None, None))
        self.last_w = {}
        self.readers = {}


class Arena:
    def __init__(self, nc, stack, name, nbytes):
        self.t = stack.enter_context(nc.sbuf_tensor(name, [128, nbytes // 4], F32))
        self.cap = nbytes // 4
        self.off = 0
        self.peak = 0

    def reset(self):
        self.peaks = getattr(self, "peaks", [])
        self.peaks.append(self.off * 4)
        self.off = 0

    def alloc_at(self, off, shape, dtype):
        save = self.off
        self.off = off
        ap = self.alloc(shape, dtype)
        self.off = max(save, self.off)
        return ap

    def alloc(self, shape, dtype):
        n = 1
        for s in shape:
            n *= s
        n4 = n if dtype == F32 else (n + 1) // 2
        assert self.off + n4 <= self.cap, f"arena overflow {self.off + n4} > {self.cap}"
        ap = self.t[:, self.off:self.off + n4]
        self.off += n4
        self.peak = max(self.peak, self.off)
        if dtype != F32:
            ap = ap.bitcast(dtype)
        if len(shape) == 2:
            ap = ap.rearrange("p (a b) -> p a b", a=shape[0])
        elif len(shape) == 3:
            ap = ap.rearrange("p (a b c) -> p a b c", a=shape[0], b=shape[1])
        return ap


def f_act(out, in_, func, **kw):
    return lambda e: e.activation(out=out, in_=in_, func=func, **kw)


def f_tt(out, in0, in1, op):
    return lambda e: e.tensor_tensor(out=out, in0=in0, in1=in1, op=op)


def f_ts(out, in0, s1, s2, op0, op1=None):
    if op1 is None:
        return lambda e: e.tensor_scalar(out=out, in0=in0, scalar1=s1, scalar2=None, op0=op0)
    return lambda e: e.tensor_scalar(out=out, in0=in0, scalar1=s1, scalar2=s2, op0=op0, op1=op1)


def f_stt(out, in0, scalar, in1, op0, op1):
    return lambda e: e.scalar_tensor_tensor(out=out, in0=in0, scalar=scalar, in1=in1, op0=op0, op1=op1)


def f_copy(out, in_):
    return lambda e: e.tensor_copy(out=out, in_=in_)


def f_dma(out, in_, **kw):
    return lambda e: e.dma_start(out=out, in_=in_, **kw)


def f_mms(mms):
    def fn(e):
        ins = None
        for (o, l, r, st, sp) in mms:
            ins = e.matmul(o, lhsT=l, rhs=r, start=st, stop=sp)
        return ins
    return fn


def f_trs(trs):
    def fn(e):
        ins = None
        for (o, i, idn) in trs:
            ins = e.transpose(o, i, idn)
        return ins
    return fn


D_MODEL = 2048
D_IN = 15888
TT = 256
COL = dict(a_q=0, a_f=1024, a_i=2048, a_g=3072, b_q=4096, b_k=5120, b_v=5376, b_g=5632,
           c_z=6656, c_x=7680, c_B=8704, c_C=9216, c_dt=9728, g_a=9744, g_b=11792, g_c=13840)
EPS = 1e-6
PAST_LEN = 4096
ATTN_SCALE = 64 ** -0.5


class Builder:
    def stop(self, name):
        if self.cfg.get("stop") == name:
            raise StopBuild()

    def __init__(self, cfg):
        self.cfg = cfg
        self.NPS, self.SEQ, self.NSS, self.DEPTH = cfg["NPS"], cfg["SEQ"], cfg["NSS"], cfg["DEPTH"]
        self.dbg = set(cfg.get("dbg", ()))
        self.dbg_out = {}
        self.NWB = cfg.get("NWB", 3)
        self.nc = bass.Bass("TRN2", target_bir_lowering=False)

    def din(self, name, shape, dt=F32):
        return self.nc.dram_tensor(name, list(shape), dt, kind="ExternalInput").ap()

    def dout(self, name, shape, dt=F32):
        return self.nc.dram_tensor(name, list(shape), dt, kind="ExternalOutput").ap()

    def dscr(self, name, shape, dt):
        return self.nc.dram_tensor(name, list(shape), dt, kind="Internal").ap()

    def sb(self, name, shape, dt):
        return self.stack.enter_context(self.nc.sbuf_tensor(name, list(shape), dt))

    def ps_alloc(self, nhalf):
        nb = 2 if nhalf > 2 else 1
        p = self.ps_ptr
        if p % nb:
            p += nb - p % nb
        if p + nb > 8:
            p = 0
        self.ps_ptr = (p + nb) % 8
        t = self.pp[p // 2]
        c0 = (p % 2) * 512
        return t[:, c0:c0 + 256 * nhalf], [f"ps{p + i}" for i in range(nb)]

    def dump(self, name, ap, keys, shape, dt=F32):
        if name not in self.dbg:
            return
        i = self.dbg_out.get(name, 0)
        self.dbg_out[name] = i + 1
        d = self.dout(f"dbg_{name}_{i}", shape, dt)
        self.S.op("pool", f_dma(d, ap), r=keys, dma=True)

    def wplan_layer(self, l):
        P = []
        for hh in range(2):
            P.append(("in", l, COL["a_f"] + hh * 512, 512))
            P.append(("in", l, COL["a_q"] + hh * 512, 512))
        for hh in range(2):
            P.append(("in", l, COL["a_i"] + hh * 512, 512))
        for hh in range(2):
            P.append(("in", l, COL["a_g"] + hh * 512, 512))
        for hh in range(2):
            P.append(("in", l, COL["b_q"] + hh * 512, 512))
        P.append(("bk", l, COL["b_k"], 256))
        P.append(("in", l, COL["b_v"], 256))
        for hh in range(2):
            P.append(("in", l, COL["b_g"] + hh * 512, 512))
        for hh in range(4):
            P.append(("in", l, COL["c_x"] + hh * 512, 512))
        P.append(("in", l, COL["c_dt"], 16))
        for hh in range(2):
            P.append(("in", l, COL["c_z"] + hh * 512, 512))
        for g in range(4):
            for nm in ("g_a", "g_b", "g_c"):
                P.append(("in", l, COL[nm] + g * 512, 512))
            for which in range(3):
                P.append(("br", l, which, g * 512))
        for g in range(4):
            P.append(("out", l, g * 512, 512))
        return P

    def w_issue(self, i):
        spec = self.wq[i]
        b = i % self.NWB
        buf = self.wbuf[b]
        S = self.S
        kind, l = spec[0], spec[1]
        if kind == "in":
            c0, n = spec[2], spec[3]
            src = self.wi_bf[l].rearrange("(kc p) e -> p kc e", p=128)[:, :, c0:c0 + n]
            dst = buf[:, 0:16 * n].rearrange("p (k c) -> p k c", k=16)
            deps = [f"cv_in{l}_{k}" for k in range(c0 // 1024, (c0 + n - 1) // 1024 + 1)]
            S.op("sp", f_dma(dst, src), r=deps, w=[f"W{b}"], dma=True)
        elif kind == "bk":
            c0 = spec[2]
            src = self.wi_bf[l].rearrange("(kc p) e -> p kc e", p=128)[:, :, c0:c0 + 256]
            dst = buf[:, 0:16 * 512].rearrange("p (k h u d) -> p k h u d", k=16, h=4, u=2)
            deps = [f"cv_in{l}_{k}" for k in range(c0 // 1024, (c0 + 255) // 1024 + 1)]
            for h in range(4):
                for u in range(2):
                    if self.cfg.get("exp3") and (h, u) != (0, 0):
                        continue
                    S.op("sp", f_dma(dst[:, :, h, u, :], src[:, :, h * 64:(h + 1) * 64]), r=deps, w=[f"W{b}"], dma=True)
        elif kind == "br":
            which, c0 = spec[2], spec[3]
            src = self.wb_bf[which][l].rearrange("(wc p) f -> p wc f", p=128)[:, :, c0:c0 + 512]
            dst = buf[:, 0:8 * 512].rearrange("p (k c) -> p k c", k=8)
            S.op("sp", f_dma(dst, src), r=[f"cv_br{which}_{l}"], w=[f"W{b}"], dma=True)
        elif kind == "out":
            c0 = spec[2]
            src = self.wo_bf[l].rearrange("(kc p) e -> p kc e", p=128)[:, :, c0:c0 + 512]
            dst = buf[:, 0:16 * 512].rearrange("p (k c) -> p k c", k=16)
            S.op("sp", f_dma(dst, src), r=[f"cv_out{l}"], w=[f"W{b}"], dma=True)

    def w_next(self, expect_kind, hold=0):
        i = self.w_cons
        self.w_cons += 1
        while self.w_iss < min(len(self.wq), i - hold + self.NWB):
            self.w_issue(self.w_iss)
            self.w_iss += 1
        spec = self.wq[i]
        assert spec[0] == expect_kind, (spec, expect_kind)
        b = i % self.NWB
        buf = self.wbuf[b]
        if spec[0] == "in":
            n = spec[3]
            v = buf[:, 0:16 * n].rearrange("p (k c) -> p k c", k=16)
        elif spec[0] == "bk":
            v = buf[:, 0:16 * 512].rearrange("p (k c) -> p k c", k=16)
        elif spec[0] == "br":
            v = buf[:, 0:8 * 512].rearrange("p (k c) -> p k c", k=8)
        else:
            v = buf[:, 0:16 * 512].rearrange("p (k c) -> p k c", k=16)
        return v, f"W{b}"

    def projF(self, wv, wkey, c0, m, ntok=TT, tok0=0):
        ps, pk = self.ps_alloc(1)
        mms = [(ps[0:m, 0:ntok], wv[:, kc, c0:c0 + m], self.hT[:, kc, tok0:tok0 + ntok], kc == 0, kc == 15) for kc in range(16)]
        self.S.op("pe", f_mms(mms), r=[wkey] + self.hTkeys, w=pk)
        return ps, pk

    def projT(self, wv, wkey, c0, n, st):
        ps, pk = self.ps_alloc(2 if n > 256 else 1)
        mms = [(ps[:, 0:n], self.hT[:, kc, st * 128:(st + 1) * 128], wv[:, kc, c0:c0 + n], kc == 0, kc == 15) for kc in range(16)]
        self.S.op("pe", f_mms(mms), r=[wkey] + self.hTkeys, w=pk)
        return ps, pk

    def build(self):
        nc = self.nc
        NPS, SEQ, NSS, DEPTH = self.NPS, self.SEQ, self.NSS, self.DEPTH
        with contextlib.ExitStack() as stack:
            self.stack = stack
            S = self.S = Sched(nc, stack)
            S.stopn = self.cfg.get("stopn")
            I = self.I = {}
            I["x_prompt"] = self.din("x_prompt", [NPS * SEQ, 2048])
            I["x_sample"] = self.din("x_sample", [NSS * 64, 2048])
            I["cache_k"] = self.din("cache_swa_k", [DEPTH, NSS, 128, 256])
            I["cache_v"] = self.din("cache_swa_v", [DEPTH, NSS, 128, 256])
            I["state_hgrn"] = self.din("state_hgrn", [DEPTH, NSS, 8, 128, 128])
            I["state_ssm"] = self.din("state_ssm", [DEPTH, NSS, 1024, 128])
            I["state_conv"] = self.din("state_conv", [DEPTH, NSS, 3, 2048])
            for nm, shp in (("norm_pre", [DEPTH, 2048]), ("norm_post", [DEPTH, 2048]), ("w_in", [DEPTH, 2048, D_IN]),
                            ("hgrn_lb_logits", [DEPTH, 1024]), ("hgrn_norm", [DEPTH, 128]), ("swa_sinks", [DEPTH, 16]),
                            ("conv_w", [DEPTH, 4, 2048]), ("conv_b", [DEPTH, 2048]), ("dt_bias", [DEPTH, 16]),
                            ("a_log", [DEPTH, 16]), ("d_skip", [DEPTH, 16]), ("ssm_norm", [DEPTH, 1024]),
                            ("w_branch_a", [DEPTH, 1024, 2048]), ("w_branch_b", [DEPTH, 1024, 2048]),
                            ("w_branch_c", [DEPTH, 1024, 2048]), ("w_out", [DEPTH, 2048, 2048]),
                            ("c_ident", [128, 128]), ("c_U2", [128, 128]), ("c_G2", [128, 128]), ("c_Pm", [128, 128]),
                            ("c_cos", [128, SEQ + 64]), ("c_sin", [128, SEQ + 64])):
                I[nm] = self.din(nm, shp)
            O = self.O = {}
            O["y_prompt"] = self.dout("y_prompt", [NPS * SEQ, 2048])
            O["y_sample"] = self.dout("y_sample", [NSS * 64, 2048])
            for sfx, nb in (("p", NPS), ("s", NSS)):
                O["nk_" + sfx] = self.dout("nk_" + sfx, [DEPTH, nb, 128, 256])
                O["nv_" + sfx] = self.dout("nv_" + sfx, [DEPTH, nb, 128, 256])
                O["nh_" + sfx] = self.dout("nh_" + sfx, [DEPTH, nb, 8, 128, 128])
                O["ns_" + sfx] = self.dout("ns_" + sfx, [DEPTH, nb, 1024, 128])
                O["nc_" + sfx] = self.dout("nc_" + sfx, [DEPTH, nb, 3, 2048])
            self.wi_bf = [self.dscr(f"wi_bf{l}", [2048, D_IN], BF16) for l in range(DEPTH)]
            self.wb_bf = [[self.dscr(f"wb_bf{w}_{l}", [1024, 2048], BF16) for l in range(DEPTH)] for w in range(3)]
            self.wo_bf = [self.dscr(f"wo_bf{l}", [2048, 2048], BF16) for l in range(DEPTH)]

            sb = self.sb
            self.X = sb("X", [128, 2, 2048], F32)
            self.hT = sb("hT", [128, 16, 256], BF16)
            self.hTkeys = [f"hT{k}" for k in range(16)]
            self.wbuf = [sb(f"wbuf{b}", [128, 8192], BF16) for b in range(self.NWB)]
            self.yT = [sb(f"yT{i}", [128, 8, 256], BF16) for i in range(3)]
            self.mergedT = sb("mergedT", [128, 16, 256], BF16)
            self.Sh = [sb(f"Sh{l}", [128, 8, 128], F32) for l in range(DEPTH)]
            shb = sb("Shb", [128, 8, 128], BF16)
            self.Shb = [shb for l in range(DEPTH)]
            self.Ssm = [sb(f"Ssm{l}", [128, 1024], F32) for l in range(DEPTH)]
            ssmb = sb("Ssmb", [128, 1024], BF16)
            self.Ssmb = [ssmb for l in range(DEPTH)]
            self.cctx = [sb(f"cctx{l}", [128, 16, 3], F32) for l in range(DEPTH)]
            self.KT = [sb(f"KT{l}", [128, 4, 384], BF16) for l in range(DEPTH)]
            self.Vt = [sb(f"Vt{l}", [128, 3, 256], BF16) for l in range(DEPTH)]
            self.ssmn_bc = sb("ssmn_bc", [128, 1024], F32)
            self.hgn_bc = sb("hgn_bc", [128, 128], F32)
            self.dskip_bc = sb("dskip_bc", [128, 16], F32)
            self.dtb_bc = sb("dtb_bc", [128, 16], F32)
            self.nA_bc = sb("nA_bc", [128, 16], F32)
            self.identf = sb("identf", [128, 128], F32)
            self.identb = sb("identb", [128, 128], BF16)
            self.U2 = sb("U2", [128, 128], F32)
            self.G2 = sb("G2", [128, 128], F32)
            self.Pm = sb("Pm", [128, 128], F32)
            self.Pmb = sb("Pmb", [128, 128], BF16)
            self.ones = sb("ones", [128, 128], F32)
            self.onesb = sb("onesb", [128, 128], BF16)
            self.cosT = sb("cosT", [128, 256], F32)
            self.sinT = sb("sinT", [128, 256], F32)
            self.npreT = sb("npreT", [128, DEPTH, 16], F32)
            self.lbT = sb("lbT", [128, DEPTH, 8], F32)
            self.omlbT = sb("omlbT", [128, DEPTH, 8], F32)
            self.elb = sb("elb", [128, DEPTH + 2, 8], F32)
            self.convw = sb("convw", [128, DEPTH, 16, 4], F32)
            self.convb = sb("convb", [128, DEPTH, 16], F32)
            self.esink = sb("esink", [128, DEPTH, 8], F32)
            self.epsb = sb("epsb", [128, 1], F32)
            self.pp = [stack.enter_context(nc.psum_tensor(f"pp{i}", [128, 1024], F32)) for i in range(4)]
            self.ps_ptr = 0
            self.arena = Arena(nc, stack, "arena", self.cfg.get("ARENA", 75 * 1024))

            self.setup()
            tiles = []
            for s in range(NPS):
                for j in range(SEQ // TT):
                    tiles.append(("p", s, j))
            for q0 in range(0, NSS, 4):
                tiles.append(("s", q0, 0))
            self.wq = []
            for _t in tiles:
                for l in range(DEPTH):
                    self.wq += self.wplan_layer(l)
            self.w_cons = 0
            self.w_iss = 0
            try:
                for ti, t in enumerate(tiles):
                    self.run_tile(t)
            except StopBuild:
                pass
            S.stopn = None
            for _i in range(self.cfg.get("delay", 0) or 0):
                de = self.cfg.get("delayeng", "dve")
                if de == "dve":
                    S.op("dve", f_copy(self.X[:, 1, :], self.X[:, 0, :]), r=["dlyX0"], w=["dlyX1"])
                elif de == "act":
                    S.op("act", f_act(self.X[:, 1, :], self.X[:, 0, :], AF.Copy), r=["dlyX0"], w=["dlyX1"])
                elif de == "actw":
                    S.op("act", f_act(self.X[:, 1, :], self.X[:, 0, :], AF.Copy), r=["X0", "hT0"], w=["dlyX1"])
                elif de == "pool":
                    S.op("pool", f_copy(self.X[:, 1, :], self.X[:, 0, :]), r=["dlyX0"], w=["dlyX1"])
            print("nops", S.nops, flush=True)
            fin = []
            for q in S.dsems:
                for i in range(S.NDSEM):
                    if S.dval[q][i] > 0:
                        fin.append(((q, i), S.dval[q][i], q))
            S.final_wait("sp", fin)
            S.emit()
        return nc

    def setup(self):
        S, I = self.S, self.I
        DEPTH = self.DEPTH
        for nm, t in (("c_ident", self.identf), ("c_U2", self.U2), ("c_G2", self.G2), ("c_Pm", self.Pm)):
            S.op("sp", f_dma(t[:], I[nm][:, :]), w=[t.name], dma=True)
        S.op("dve", f_copy(self.identb[:], self.identf[:]), r=["identf"], w=["identb"])
        S.op("dve", f_copy(self.Pmb[:], self.Pm[:]), r=["Pm"], w=["Pmb"])
        S.op("pool", lambda e: e.memset(self.ones[:], 1.0), w=["ones"])
        S.op("pool", lambda e: e.memset(self.onesb[:], 1.0), w=["onesb"])
        S.op("pool", lambda e: e.memset(self.epsb[:], EPS), w=["epsb"])
        with self.nc.allow_non_contiguous_dma(reason="tiny param loads"):
            for l in range(DEPTH):
                S.op("sp", f_dma(self.npreT[:, l, :], I["norm_pre"][l].rearrange("(kc p) -> p kc", p=128), allow_slow_non_contiguous=True), w=["npreT"], dma=True)
                S.op("sp", f_dma(self.elb[:, l, :], I["hgrn_lb_logits"][l].rearrange("(h k) -> k h", k=128), allow_slow_non_contiguous=True), w=["elb"], dma=True)
                for jj in range(4):
                    S.op("sp", f_dma(self.convw[:, l, :, jj], I["conv_w"][l, jj].rearrange("(cc p) -> p cc", p=128), allow_slow_non_contiguous=True), w=["convw"], dma=True)
                S.op("sp", f_dma(self.convb[:, l, :], I["conv_b"][l].rearrange("(cc p) -> p cc", p=128), allow_slow_non_contiguous=True), w=["convb"], dma=True)
                sv = I["swa_sinks"][l].rearrange("(cc u) -> u cc", u=2)
                for u in range(2):
                    S.op("sp", f_dma(self.esink[u * 64:(u + 1) * 64, l, :], sv[u].partition_broadcast(64), allow_slow_non_contiguous=True), w=["esink"], dma=True)
        S.op("act", f_act(self.esink[:], self.esink[:], AF.Exp), r=["esink"], w=["esink"])
        e = self.elb
        S.op("act", f_act(e[:, 0:DEPTH, :], e[:, 0:DEPTH, :], AF.Exp), r=["elb"], w=["elb"])
        tot, cum = e[:, DEPTH, :], e[:, DEPTH + 1, :]
        S.op("dve", f_copy(tot, e[:, 0, :]), r=["elb"], w=["elb"])
        for l in range(1, DEPTH):
            S.op("dve", f_tt(tot, tot, e[:, l, :], ALU.add), r=["elb"], w=["elb"])
        S.op("dve", lambda en: en.reciprocal(out=tot, in_=tot), r=["elb"], w=["elb"])
        S.op("dve", lambda en: en.memset(cum, 0.0), r=["elb"], w=["elb"])
        S.op("dve", lambda en: en.memset(self.lbT[:, 0, :], 0.0), w=["lb"])
        for l in range(1, DEPTH):
            S.op("dve", f_tt(cum, cum, e[:, l, :], ALU.add), r=["elb"], w=["elb"])
            S.op("dve", f_tt(self.lbT[:, l, :], cum, tot, ALU.mult), r=["elb"], w=["lb"])
        S.op("dve", f_ts(self.omlbT[:], self.lbT[:], -1.0, 1.0, ALU.mult, ALU.add), r=["lb"], w=["lb"])
        for l in range(DEPTH):
            nblk = (D_IN + 1023) // 1024
            for k in range(nblk):
                c0, c1 = k * 1024, min(D_IN, (k + 1) * 1024)
                S.op("pool", f_dma(self.wi_bf[l][:, c0:c1], I["w_in"][l][:, c0:c1]), w=[f"cv_in{l}_{k}"], dma=True)
            for w, nm in enumerate(("w_branch_a", "w_branch_b", "w_branch_c")):
                S.op("pool", f_dma(self.wb_bf[w][l][:, :], I[nm][l][:, :]), w=[f"cv_br{w}_{l}"], dma=True)
            S.op("pool", f_dma(self.wo_bf[l][:, :], I["w_out"][l][:, :]), w=[f"cv_out{l}"], dma=True)

    def load_LC(self, l):
        S, I = self.S, self.I
        for t, nm in ((self.ssmn_bc, "ssm_norm"), (self.hgn_bc, "hgrn_norm"),
                      (self.dskip_bc, "d_skip"), (self.dtb_bc, "dt_bias"), (self.nA_bc, "a_log")):
            S.op("sp", f_dma(t[:], I[nm][l].partition_broadcast(128)), w=[t.name], dma=True)
        S.op("act", f_act(self.nA_bc[:], self.nA_bc[:], AF.Exp), r=["nA_bc"], w=["nA_bc"])
        S.op("dve", f_ts(self.nA_bc[:], self.nA_bc[:], -1.0, None, ALU.mult), r=["nA_bc"], w=["nA_bc"])

    def run_tile(self, t):
        S, I, O = self.S, self.I, self.O
        kind, a, j = t
        SEQ = self.SEQ
        if kind == "p":
            row0 = a * SEQ + j * TT
            xsrc, ydst = I["x_prompt"], O["y_prompt"]
            pos0 = j * TT
        else:
            row0 = a * 64
            xsrc, ydst = I["x_sample"], O["y_sample"]
        S.barrier()
        for st in range(2):
            S.op("sp", f_dma(self.X[:, st, :], xsrc[row0 + st * 128: row0 + (st + 1) * 128, :]), w=[f"X{st}"], dma=True)
        if kind == "p":
            S.op("sp", f_dma(self.cosT[:], I["c_cos"][:, pos0:pos0 + TT]), w=["cosT"], dma=True)
            S.op("sp", f_dma(self.sinT[:], I["c_sin"][:, pos0:pos0 + TT]), w=["sinT"], dma=True)
        else:
            for q in range(4):
                S.op("sp", f_dma(self.cosT[:, q * 64:(q + 1) * 64], I["c_cos"][:, SEQ:SEQ + 64]), w=["cosT"], dma=True)
                S.op("sp", f_dma(self.sinT[:, q * 64:(q + 1) * 64], I["c_sin"][:, SEQ:SEQ + 64]), w=["sinT"], dma=True)
        for l in range(self.DEPTH):
            self.T = dict(kind=kind, a=a, j=j, l=l, first=(kind == "p" and j == 0),
                          last=(kind == "p" and j == SEQ // TT - 1), sample=(kind == "s"))
            self.stop("setup")
            self.phase0()
            self.stop("p0")
            self.phaseA()
            self.stop("A")
            self.phaseB()
            self.stop("B")
            self.phaseC()
            self.stop("C")
            self.phaseM()
            self.stop("M")
        for st in range(2):
            S.op("pool", f_dma(ydst[row0 + st * 128: row0 + (st + 1) * 128, :], self.X[:, st, :]), r=[f"X{st}"], dma=True)

    def phase0(self):
        S, A, l = self.S, self.arena, self.T["l"]
        S.barrier()
        A.reset()
        if self.DEPTH > 1 or (self.T["kind"] == "p" and self.T["a"] == 0 and self.T["j"] == 0):
            self.load_LC(l)
        xn = A.alloc([2, 2048], BF16)
        junk = A.alloc([2048], BF16)
        ssq = A.alloc([4], F32)
        for st in range(2):
            S.op("act", f_act(junk, self.X[:, st, :], AF.Square, accum_out=ssq[:, st:st + 1]), r=[f"X{st}"], w=["junk", f"ssq{st}"])
            S.op("act", f_act(ssq[:, 2 + st:3 + st], ssq[:, st:st + 1], AF.Sqrt, scale=1.0 / 2048, bias=self.epsb[:, 0:1]),
                 r=[f"ssq{st}", "epsb"], w=[f"rs{st}"])
            S.op("dve", (lambda o: (lambda e: e.reciprocal(out=o, in_=o)))(ssq[:, 2 + st:3 + st]), r=[f"rs{st}"], w=[f"rs{st}"])
            S.op("act", f_act(xn[:, st, :], self.X[:, st, :], AF.Copy, scale=ssq[:, 2 + st:3 + st]), r=[f"X{st}", f"rs{st}"], w=[f"xn{st}"])
        for kc in range(16):
            ps, pk = self.ps_alloc(1)
            psb = ps.bitcast(BF16)
            trs = [(psb[:, st * 128:(st + 1) * 128], xn[:, st, kc * 128:(kc + 1) * 128], self.identb[:]) for st in range(2)]
            S.op("pe", f_trs(trs), r=["xn0", "xn1", "identb"], w=pk)
            if kc % 2 == 0:
                S.op("dve", f_ts(self.hT[:, kc, :], psb[:, 0:256], self.npreT[:, l, kc:kc + 1], None, ALU.mult), r=pk + ["npreT"], w=[f"hT{kc}"])
            else:
                S.op("act", f_act(self.hT[:, kc, :], psb[:, 0:256], AF.Copy, scale=self.npreT[:, l, kc:kc + 1]), r=pk + ["npreT"], w=[f"hT{kc}"])
        self.dump("hT", self.hT[:], self.hTkeys, [128, 16, 256], BF16)

    def phaseA(self):
        S, A, T = self.S, self.arena, self.T
        l = T["l"]
        S.barrier()
        A.reset()
        qT = A.alloc([8, 256], BF16)
        kT = A.alloc([8, 256], BF16)
        ebl = A.alloc([8, 4], F32)
        tmp = [{n: A.alloc([256], F32) for n in ("f", "k", "lf", "b", "eb", "enb", "sq")} for _ in range(2)]
        vtok = A.alloc([2, 1024], BF16)
        gtok = A.alloc([2, 1024], F32)
        otok = A.alloc([2, 1024], F32)
        ktok = [A.alloc([1024], BF16) for _ in range(2)]
        ATs = [A.alloc([512], BF16) for _ in range(2)]
        nt1 = A.alloc([1024], F32)
        ssh = A.alloc([16], F32)
        ya = A.alloc([2, 1024], BF16)
        Sh, Shb = self.Sh[l], self.Shb[l]
        if T["first"]:
            S.op("dve", lambda e: e.memset(Sh[:], 0.0), w=["Sh"])
            S.op("pool", lambda e: e.memset(Shb[:], 0.0), w=["Shb"])
        elif not T["sample"]:
            S.op("act", f_act(Shb[:], Sh[:], AF.Copy), r=["Sh"], w=["Shb"])
        qk = [f"qT{h}" for h in range(8)]
        kk = [f"kT{h}" for h in range(8)]
        for hh in range(2):
            wf, wfk = self.w_next("in")
            wq_, wqk = self.w_next("in", hold=1)
            for h4 in range(4):
                hd = hh * 4 + h4
                t = tmp[hd % 2]
                x = f"_{hd % 2}"
                ps, pk = self.projF(wf, wfk, h4 * 128, 128)
                S.op("act", f_act(t["f"], ps, AF.Sigmoid), r=pk, w=["tf" + x])
                S.op("dve", f_ts(t["f"], t["f"], self.omlbT[:, l, hd:hd + 1], self.lbT[:, l, hd:hd + 1], ALU.mult, ALU.add),
                     r=["tf" + x, "lb"], w=["tf" + x])
                S.op("dve", f_ts(t["k"], t["f"], -1.0, 1.0, ALU.mult, ALU.add), r=["tf" + x], w=["tk" + x])
                S.op("act", f_act(t["lf"], t["f"], AF.Ln), r=["tf" + x], w=["tlf" + x])
                for c in range(4):
                    cs = slice(c * 64, (c + 1) * 64)
                    S.op("dve", (lambda o, d1: (lambda e: e.tensor_tensor_scan(out=o, data0=self.ones[:, 0:64], data1=d1, initial=0.0,
                                                                                op0=ALU.mult, op1=ALU.add)))(t["b"][:, cs], t["lf"][:, cs]),
                         r=["tlf" + x, "ones"], w=["tb" + x])
                S.op("act", f_act(t["eb"], t["b"], AF.Exp), r=["tb" + x], w=["teb" + x])
                S.op("act", f_act(t["enb"], t["b"], AF.Exp, scale=-1.0), r=["tb" + x], w=["tenb" + x])
                if not self.cfg.get("exp2"):
                  S.op("act", f_act(ebl[:, hd, :], (t["eb"][:, 0:4] if self.cfg.get("exp1") else t["eb"].rearrange("p (c s) -> p c s", s=64)[:, :, 63]), AF.Copy), r=["teb" + x], w=["ebl"])
                psq, pkq = self.projF(wq_, wqk, h4 * 128, 128)
                S.op("act", f_act(t["sq"], psq, AF.Silu), r=pkq, w=["tsq" + x])
                if hd == 0:
                    for nm in ("f", "k", "lf", "b", "eb", "enb", "sq"):
                        self.dump("t_" + nm, t[nm], ["tf" + x, "tk" + x, "tlf" + x, "tb" + x, "teb" + x, "tenb" + x, "tsq" + x], [128, 256])
                S.op("dve", f_tt(qT[:, hd, :], t["sq"], t["eb"], ALU.mult), r=["tsq" + x, "teb" + x], w=[qk[hd]])
                S.op("dve", f_tt(kT[:, hd, :], t["k"], t["enb"], ALU.mult), r=["tk" + x, "tenb" + x], w=[kk[hd]])
        self.stop("A1")
        for hh in range(2):
            wv, wk = self.w_next("in")
            for st in range(2):
                ps, pk = self.projT(wv, wk, 0, 512, st)
                S.op("act", f_act(vtok[:, st, hh * 512:(hh + 1) * 512], ps, AF.Copy), r=pk, w=[f"vtok{st}"])
        for hh in range(2):
            wv, wk = self.w_next("in")
            for st in range(2):
                ps, pk = self.projT(wv, wk, 0, 512, st)
                S.op("act", f_act(gtok[:, st, hh * 512:(hh + 1) * 512], ps, AF.Silu), r=pk, w=[f"gtok{st}"])
        self.dump("qT", qT, qk, [128, 8, 256], BF16)
        self.dump("kT", kT, kk, [128, 8, 256], BF16)
        self.stop("A2")
        def a_s1(c):
            st, pb = c // 2, (c % 2) * 64
            P = slice(pb, pb + 64)
            cs = slice(c * 64, (c + 1) * 64)
            kt = ktok[c % 2]
            at = ATs[c % 2]
            ps, pk = self.ps_alloc(2)
            psb = ps.bitcast(BF16)
            trs = [(psb[P, hd * 128:(hd + 1) * 128], kT[:, hd, cs], self.identb[:]) for hd in range(8)]
            S.op("pe", f_trs(trs), r=kk + ["identb"], w=pk)
            S.op("act", f_act(kt[P, :], psb[P, :], AF.Copy), r=pk, w=[f"ktok{c % 2}"])
            ps2, pk2 = self.ps_alloc(2)
            mms = [(ps2[P, hd * 64:(hd + 1) * 64], kT[:, hd, cs], qT[:, hd, cs], True, True) for hd in range(8)]
            S.op("pe", f_mms(mms), r=kk + qk, w=pk2)
            S.op("dve", f_tt(at[P, :].rearrange("p (h t) -> p h t", h=8), ps2[P, :].rearrange("p (h t) -> p h t", h=8),
                             self.U2[P, 0:64].rearrange("p (o t) -> p o t", o=1).broadcast_to([64, 8, 64]), ALU.mult),
                 r=pk2 + ["U2"], w=[f"AT{c % 2}"])

        def a_s2(c):
            st, pb = c // 2, (c % 2) * 64
            P = slice(pb, pb + 64)
            cs = slice(c * 64, (c + 1) * 64)
            kt = ktok[c % 2]
            at = ATs[c % 2]
            if T["sample"]:
                seq = T["a"] + c
                S.op("sp", f_dma(Sh[:], self.I["state_hgrn"][l, seq].rearrange("h k v -> k h v")), w=["Sh"], dma=True)
                S.op("act", f_act(Shb[:], Sh[:], AF.Copy), r=["Sh"], w=["Shb"])
            ps3, pk3 = self.ps_alloc(4)
            mms = []
            for hd in range(8):
                o_ap = ps3[P, hd * 128:(hd + 1) * 128]
                mms.append((o_ap, at[P, hd * 64:(hd + 1) * 64], vtok[P, st, hd * 128:(hd + 1) * 128], True, False))
                mms.append((o_ap, qT[:, hd, cs], Shb[:, hd, :], False, True))
            S.op("pe", f_mms(mms), r=[f"AT{c % 2}", f"vtok{st}", "Shb"] + qk, w=pk3)
            S.op("act", f_act(otok[P, st, :], ps3[P, :], AF.Copy), r=pk3, w=[f"otok{st}"])
            ps4, pk4 = self.ps_alloc(4)
            mms = [(ps4[:, hd * 128:(hd + 1) * 128], kt[P, hd * 128:(hd + 1) * 128], vtok[P, st, hd * 128:(hd + 1) * 128], True, True)
                   for hd in range(8)]
            S.op("pe", f_mms(mms), r=[f"ktok{c % 2}", f"vtok{st}"], w=pk4)
            Sf = Sh[:].rearrange("p h v -> p (h v)")
            S.op("dve", f_tt(Sf, ps4, Sf, ALU.add), r=pk4 + ["Sh"], w=["Sh"])
            S.op("dve", f_tt(Sh[:], Sh[:], ebl[:, :, c:c + 1].broadcast_to([128, 8, 128]), ALU.mult), r=["Sh", "ebl"], w=["Sh"])
            S.op("act", f_act(Shb[:], Sh[:], AF.Copy), r=["Sh"], w=["Shb"])
            if T["sample"] or (T["last"] and c == 3):
                dst = (self.O["nh_s"][l, T["a"] + c] if T["sample"] else self.O["nh_p"][l, T["a"]])
                S.op("pool", f_dma(dst.rearrange("h k v -> k h v"), Sh[:]), r=["Sh"], dma=True)

        a_s1(0)
        for c in range(4):
            if c + 1 < 4:
                a_s1(c + 1)
            a_s2(c)
        self.dump("otok", otok, ["otok0", "otok1"], [128, 2, 1024])
        for st in range(2):
            o3 = otok[:, st, :]
            n3 = nt1.rearrange("p (h v) -> p h v", h=8)
            S.op("dve", f_tt(nt1, o3, o3, ALU.mult), r=[f"otok{st}"], w=["nt1"])
            S.op("dve", (lambda o, i: (lambda e: e.tensor_reduce(out=o, in_=i, axis=AX.X, op=ALU.add)))(ssh[:, 0:8], n3), r=["nt1"], w=["ssh"])
            S.op("act", f_act(ssh[:, 8:16], ssh[:, 0:8], AF.Sqrt, scale=1.0 / 128, bias=self.epsb[:, 0:1]), r=["ssh", "epsb"], w=["ssh2"])
            S.op("dve", (lambda o: (lambda e: e.reciprocal(out=o, in_=o)))(ssh[:, 8:16]), r=["ssh2"], w=["ssh2"])
            S.op("dve", f_tt(n3, o3.rearrange("p (h v) -> p h v", h=8),
                             ssh[:, 8:16].rearrange("p (h o) -> p h o", o=1).broadcast_to([128, 8, 128]), ALU.mult),
                 r=[f"otok{st}", "ssh2"], w=["nt1"])
            S.op("dve", f_tt(n3, n3, self.hgn_bc[:].rearrange("p (o v) -> p o v", o=1).broadcast_to([128, 8, 128]), ALU.mult),
                 r=["nt1", "hgn_bc"], w=["nt1"])
            S.op("dve", f_tt(ya[:, st, :], nt1, gtok[:, st, :], ALU.mult), r=["nt1", f"gtok{st}"], w=[f"ya{st}"])
        self.tr_to_yT(ya, ["ya0", "ya1"], 0)

    def tr_to_yT(self, y, ykeys, bi):
        S = self.S
        for wc in range(8):
            ps, pk = self.ps_alloc(1)
            psb = ps.bitcast(BF16)
            trs = [(psb[:, st * 128:(st + 1) * 128], y[:, st, wc * 128:(wc + 1) * 128], self.identb[:]) for st in range(2)]
            S.op("pe", f_trs(trs), r=ykeys + ["identb"], w=pk)
            if wc % 2 == 0:
                S.op("act", f_act(self.yT[bi][:, wc, :], psb[:, 0:256], AF.Copy), r=pk, w=[f"yT{bi}"])
            else:
                S.op("dve", f_copy(self.yT[bi][:, wc, :], psb[:, 0:256]), r=pk, w=[f"yT{bi}"])
        self.dump(f"yT{bi}", self.yT[bi][:], [f"yT{bi}"], [128, 8, 256], BF16)

    def rotary(self, ps, pk, t, x, out_ap, out_key):
        S = self.S
        S.op("act", f_act(t["x"], ps, AF.Copy), r=pk, w=["rx" + x])
        S.op("act", f_act(t["xb"], ps, AF.Copy), r=pk, w=["rxb" + x])
        ps2, pk2 = self.ps_alloc(1)
        S.op("pe", f_mms([(ps2, self.Pmb[:], t["xb"], True, True)]), r=["rxb" + x, "Pmb"], w=pk2)
        S.op("dve", f_tt(t["t1"], t["x"], self.cosT[:], ALU.mult), r=["rx" + x, "cosT"], w=["rt1" + x])
        S.op("dve", f_tt(t["t2"], ps2, self.sinT[:], ALU.mult), r=pk2 + ["sinT"], w=["rt2" + x])
        S.op("dve", f_tt(out_ap, t["t1"], t["t2"], ALU.add), r=["rt1" + x, "rt2" + x], w=[out_key])

    def phaseB(self):
        S, A, T, I, O = self.S, self.arena, self.T, self.I, self.O
        l = T["l"]
        S.barrier()
        A.reset()
        qr = A.alloc([8, 256], BF16)
        krf = A.alloc([4, 256], F32)
        sg = A.alloc([8, 256], F32)
        vf = A.alloc([2, 256], F32)
        xt = [{n: A.alloc([256], F32) for n in ("x", "t1", "t2")} for _ in range(2)]
        for _t in xt:
            _t["xb"] = A.alloc([256], BF16)
        pTs = [[A.alloc([1024], BF16) for _ in range(2)] for _ in range(2)]
        den = A.alloc([512], F32)
        ob = A.alloc([512], F32)
        ck = A.alloc([4, 2, 64], F32)
        cv = A.alloc([256], F32)
        kout = A.alloc([4, 128], F32)
        KT, Vt = self.KT[l], self.Vt[l]
        qrk = [f"qr{c}" for c in range(8)]
        for hh in range(2):
            wv, wk = self.w_next("in")
            for c4 in range(4):
                cc = hh * 4 + c4
                ps, pk = self.projF(wv, wk, c4 * 128, 128)
                self.rotary(ps, pk, xt[cc % 2], f"_{cc % 2}", qr[:, cc, :], qrk[cc])
        wv, wk = self.w_next("bk")
        for h in range(4):
            ps, pk = self.projF(wv, wk, h * 128, 128)
            self.rotary(ps, pk, xt[h % 2], f"_{h % 2}", krf[:, h, :], f"krf{h}")
            S.op("act", f_act(KT[:, h, 128:384], krf[:, h, :], AF.Copy), r=[f"krf{h}"], w=["KTcur"])
        wv, wk = self.w_next("in")
        for st in range(2):
            ps, pk = self.projT(wv, wk, 0, 256, st)
            S.op("act", f_act(vf[:, st, :], ps[:, 0:256], AF.Copy), r=pk, w=[f"vf{st}"])
            S.op("dve", f_copy(Vt[:, 1 + st, :], vf[:, st, :]), r=[f"vf{st}"], w=[f"Vt{1 + st}"])
        for hh in range(2):
            wv, wk = self.w_next("in")
            for c4 in range(4):
                cc = hh * 4 + c4
                ps, pk = self.projF(wv, wk, c4 * 128, 128)
                S.op("act", f_act(sg[:, cc, :], ps, AF.Silu), r=pk, w=["sg"])
        self.dump("qr", qr, qrk, [128, 8, 256], BF16)
        self.dump("krf", krf, [f"krf{h}" for h in range(4)], [128, 4, 256])
        seg_of = {}

        def b_s1(c):
            cs = slice(c * 64, (c + 1) * 64)
            pT = pTs[c % 2]
            if T["sample"]:
                seq = T["a"] + c
                ckv = I["cache_k"][l, seq].rearrange("k (h d) -> k h d", h=4)
                for u in range(2):
                    S.op("sp", f_dma(ck[:, :, u, :], ckv), w=["ck"], dma=True)
                S.op("sp", f_dma(cv, I["cache_v"][l, seq]), w=["cv"], dma=True)
                for h in range(4):
                    ps, pk = self.ps_alloc(1)
                    S.op("pe", f_trs([(ps[:, 0:128], ck[:, h, :, :].rearrange("p u d -> p (u d)"), self.identf[:])]), r=["ck", "identf"], w=pk)
                    S.op("act", f_act(KT[:, h, 0:128], ps[:, 0:128], AF.Copy), r=pk, w=["KTprev"])
                S.op("dve", f_copy(Vt[:, 0, :], cv), r=["cv"], w=["Vt0"])
                blocks = [(0, 0, 0), (0, 64, 64), (1 + c // 2, (c % 2) * 64, 128 + c * 64)]
            else:
                blocks = []
                for bl in (c - 2, c - 1, c):
                    if T["j"] * 4 + bl < 0:
                        continue
                    if bl < 0:
                        blocks.append((0, (bl + 2) * 64, (bl + 2) * 64))
                    else:
                        blocks.append((1 + bl // 2, (bl % 2) * 64, 128 + bl * 64))
            segs = []
            for b in blocks:
                if segs and segs[-1][0] == b[0] and segs[-1][1] == 0 and segs[-1][2] == 64 and b[1] == 64:
                    segs[-1] = (b[0], 0, 128, segs[-1][3])
                else:
                    segs.append((b[0], b[1], 64, b[2]))
            vkey = {0: "Vt0", 1: "Vt1", 2: "Vt2"}
            for si, (vidx, p0, nk, kc0) in enumerate(segs):
                ps, pk = self.ps_alloc(4)
                mms = []
                for h in range(4):
                    for u in range(2):
                        U = slice(u * 64, (u + 1) * 64)
                        g = u * 4 + h
                        mms.append((ps[p0:p0 + nk, g * 128:(g + 1) * 128], KT[U, h, kc0:kc0 + nk], qr[U, 2 * h:2 * h + 2, cs], True, True))
                S.op("pe", f_mms(mms), r=["KTcur", "KTprev"] + qrk, w=pk)
                S.op("act", f_act(pT[si][p0:p0 + nk, :], ps[p0:p0 + nk, :], AF.Exp, scale=ATTN_SCALE), r=pk, w=[f"pT{c % 2}_{si}"])
            seg_of[c] = segs

        def b_s2(c):
            cs = slice(c * 64, (c + 1) * 64)
            pT = pTs[c % 2]
            segs = seg_of[c]
            vkey = {0: "Vt0", 1: "Vt1", 2: "Vt2"}
            pso, pko = self.ps_alloc(2)
            psd, pkd = self.ps_alloc(2)
            mo, md = [], []
            ns = len(segs)
            for h in range(4):
                for u in range(2):
                    U = slice(u * 64, (u + 1) * 64)
                    g = u * 4 + h
                    for si, (vidx, p0, nk, kc0) in enumerate(segs):
                        KP = slice(p0, p0 + nk)
                        mo.append((pso[U, 2 * h * 64:(2 * h + 2) * 64], Vt[KP, vidx, h * 64:(h + 1) * 64], pT[si][KP, g * 128:(g + 1) * 128], si == 0, si == ns - 1))
                        md.append((psd[U, 2 * h * 64:(2 * h + 2) * 64], self.onesb[KP, 0:64], pT[si][KP, g * 128:(g + 1) * 128], si == 0, si == ns - 1))
            rk = [f"pT{c % 2}_{si}" for si in range(ns)] + [vkey[s[0]] for s in segs]
            S.op("pe", f_mms(mo), r=rk, w=pko)
            S.op("pe", f_mms(md), r=rk + ["onesb"], w=pkd)
            d3 = den.rearrange("p (c q) -> p c q", c=8)
            S.op("dve", f_tt(d3, psd.rearrange("p (c q) -> p c q", c=8),
                             self.esink[:, l, :].rearrange("p (c o) -> p c o", o=1).broadcast_to([128, 8, 64]), ALU.add), r=pkd + ["esink"], w=["den"])
            S.op("dve", (lambda o: (lambda e: e.reciprocal(out=o, in_=o)))(den), r=["den"], w=["den"])
            S.op("dve", f_tt(ob, pso, den, ALU.mult), r=pko + ["den"], w=["ob"])
            S.op("dve", f_tt(self.yT[1][:, :, cs], ob.rearrange("p (c q) -> p c q", c=8), sg[:, :, cs], ALU.mult), r=["ob", "sg"], w=["yT1"])
            if T["sample"]:
                seq = T["a"] + c
                pb = (c % 2) * 64
                S.op("pool", f_dma(O["nk_s"][l, seq, 0:64, :], I["cache_k"][l, seq, 64:128, :]), dma=True)
                S.op("pool", f_dma(O["nv_s"][l, seq, 0:64, :], I["cache_v"][l, seq, 64:128, :]), dma=True)
                ps, pk = self.ps_alloc(2)
                S.op("pe", f_trs([(ps[0:64, h * 128:(h + 1) * 128], krf[:, h, cs], self.identf[:]) for h in range(4)]),
                     r=[f"krf{h}" for h in range(4)] + ["identf"], w=pk)
                S.op("act", f_act(kout[0:64, :, 0:64], ps[0:64, :].rearrange("p (h x) -> p h x", h=4)[:, :, 0:64], AF.Copy), r=pk, w=["kout"])
                S.op("pool", f_dma(O["nk_s"][l, seq, 64:128, :].rearrange("k (h d) -> k h d", h=4), kout[0:64, :, 0:64]), r=["kout"], dma=True)
                S.op("pool", f_dma(O["nv_s"][l, seq, 64:128, :], vf[pb:pb + 64, c // 2, :]), r=[f"vf{c // 2}"], dma=True)

        if T["sample"]:
            for c in range(4):
                b_s1(c)
                b_s2(c)
        else:
            b_s1(0)
            for c in range(4):
                if c + 1 < 4:
                    b_s1(c + 1)
                b_s2(c)
        self.dump("yT1", self.yT[1][:], ["yT1"], [128, 8, 256], BF16)
        if T["last"]:
            s = T["a"]
            ps, pk = self.ps_alloc(2)
            S.op("pe", f_trs([(ps[:, h * 128:(h + 1) * 128], krf[:, h, 128:256], self.identf[:]) for h in range(4)]),
                 r=[f"krf{h}" for h in range(4)] + ["identf"], w=pk)
            S.op("act", f_act(kout[:, :, 0:64], ps.rearrange("p (h x) -> p h x", h=4)[:, :, 0:64], AF.Copy), r=pk, w=["kout"])
            S.op("pool", f_dma(O["nk_p"][l, s].rearrange("k (h d) -> k h d", h=4), kout[:, :, 0:64]), r=["kout"], dma=True)
            S.op("pool", f_dma(O["nv_p"][l, s], vf[:, 1, :]), r=["vf1"], dma=True)
        if not T["sample"]:
            S.op("act", f_act(KT[:, :, 0:128], KT[:, :, 256:384], AF.Copy), r=["KTcur"], w=["KTprev"])
            S.op("dve", f_copy(Vt[:, 0, :], Vt[:, 2, :]), r=["Vt2"], w=["Vt0"])

    def phaseC(self):
        S, A, T, I, O = self.S, self.arena, self.T, self.I, self.O
        l = T["l"]
        S.barrier()
        A.reset()
        XC = A.alloc([8, 256], F32)
        xr = [A.alloc([268], F32) for _ in range(2)]
        acc = [A.alloc([256], F32) for _ in range(2)]
        BTb = A.alloc([4, 256], BF16)
        CTb = A.alloc([4, 256], BF16)
        xtok = A.alloc([2, 1024], F32)
        xdtb = A.alloc([2, 1024], BF16)
        Btok = A.alloc([2, 512], BF16)
        ztok = A.alloc([2, 1024], F32)
        ytok = A.alloc_at(0, [2, 1024], F32)
        dtt = A.alloc([2, 16], F32)
        lat = A.alloc([2, 16], F32)
        dtw = A.alloc([16], F32)
        ebt2 = [A.alloc([16], F32) for _ in range(2)]
        eblb2 = [A.alloc([16], F32) for _ in range(2)]
        Lmat = A.alloc([1024], F32)
        expD2 = [A.alloc([1024], F32) for _ in range(2)]
        Mb2 = [A.alloc([1024], BF16) for _ in range(2)]
        cbm2 = [A.alloc([256], F32) for _ in range(2)]
        xdtw = A.alloc([1024], BF16)
        t1 = A.alloc([1024], F32)
        nt = A.alloc([1024], F32)
        ss4 = A.alloc([8], F32)
        ycb = A.alloc([2, 1024], BF16)
        sld = nt.rearrange("p (b n) -> p b n", b=8)
        scv = A.alloc([16, 4, 3], F32)
        Ssm, Ssmb, cctx = self.Ssm[l], self.Ssmb[l], self.cctx[l]
        if T["first"]:
            S.op("dve", lambda e: e.memset(Ssm[:], 0.0), w=["Ssm"])
            S.op("pool", lambda e: e.memset(Ssmb[:], 0.0), w=["Ssmb"])
            S.op("pool", lambda e: e.memset(cctx[:], 0.0), w=["cctx"])
        elif not T["sample"]:
            S.op("act", f_act(Ssmb[:], Ssm[:], AF.Copy), r=["Ssm"], w=["Ssmb"])
        if T["sample"]:
            with self.nc.allow_non_contiguous_dma(reason="conv state"):
                for q in range(4):
                    for jj in range(3):
                        S.op("sp", f_dma(scv[:, :, q, jj], I["state_conv"][l, T["a"] + q, jj].rearrange("(cc p) -> p cc", p=128), allow_slow_non_contiguous=True), w=["scv"], dma=True)
        if T["sample"]:
            segs = [(q * 64, q * 67, 64) for q in range(4)]
        else:
            segs = [(0, 0, 256)]
        for hh in range(4):
            wv, wk = self.w_next("in")
            for c4 in range(4):
                ch = hh * 4 + c4
                b = ch % 2
                x = f"_{b}"
                ps, pk = self.projF(wv, wk, c4 * 128, 128)
                if T["sample"]:
                    x3 = xr[b].rearrange("p (q s) -> p q s", s=67)
                    S.op("act", f_act(x3[:, :, 3:67], ps.rearrange("p (q t) -> p q t", t=64), AF.Copy), r=pk, w=["xr" + x])
                    S.op("act", f_act(x3[:, :, 0:3], scv[:, ch, :, :], AF.Copy), r=["scv"], w=["xrc" + x])
                else:
                    S.op("act", f_act(xr[b][:, 3:259], ps, AF.Copy), r=pk, w=["xr" + x])
                    S.op("act", f_act(xr[b][:, 0:3], cctx[:, ch, :], AF.Copy), r=["cctx"], w=["xrc" + x])
                for (o0, i0, n) in segs:
                    S.op("dve", f_ts(acc[b][:, o0:o0 + n], xr[b][:, i0:i0 + n], self.convw[:, l, ch, 0:1], None, ALU.mult),
                         r=["xr" + x, "xrc" + x, "convw"], w=["acc" + x])
                    for jj in range(1, 4):
                        S.op("dve", f_stt(acc[b][:, o0:o0 + n], xr[b][:, i0 + jj:i0 + jj + n], self.convw[:, l, ch, jj:jj + 1], acc[b][:, o0:o0 + n],
                                          ALU.mult, ALU.add), r=["xr" + x, "xrc" + x, "convw", "acc" + x], w=["acc" + x])
                if ch < 8:
                    dst, dk = XC[:, ch, :], f"XC{ch}"
                elif ch < 12:
                    dst, dk = BTb[:, ch - 8, :], "BTb"
                else:
                    dst, dk = CTb[:, ch - 12, :], "CTb"
                S.op("act", f_act(dst, acc[b], AF.Silu, bias=self.convb[:, l, ch:ch + 1]), r=["acc" + x, "convb"], w=[dk])
                with self.nc.allow_non_contiguous_dma(reason="conv state out"):
                    if T["sample"]:
                        x3 = xr[b].rearrange("p (q s) -> p q s", s=67)
                        for q in range(4):
                            S.op("pool", f_dma(O["nc_s"][l, T["a"] + q][:, ch * 128:(ch + 1) * 128].rearrange("j p -> p j"), x3[:, q, 64:67], allow_slow_non_contiguous=True),
                                 r=["xr" + x], dma=True)
                    else:
                        S.op("act", f_act(cctx[:, ch, :], xr[b][:, 256:259], AF.Copy), r=["xr" + x], w=["cctx"])
                        if T["last"]:
                            S.op("pool", f_dma(O["nc_p"][l, T["a"]][:, ch * 128:(ch + 1) * 128].rearrange("j p -> p j"), xr[b][:, 256:259], allow_slow_non_contiguous=True),
                                 r=["xr" + x], dma=True)
        wv, wk = self.w_next("in")
        for st in range(2):
            ps, pk = self.projT(wv, wk, 0, 16, st)
            S.op("dve", f_tt(dtt[:, st, :], ps[:, 0:16], self.dtb_bc[:], ALU.add), r=pk + ["dtb_bc"], w=[f"dtt{st}"])
            S.op("act", f_act(dtt[:, st, :], dtt[:, st, :], AF.Exp), r=[f"dtt{st}"], w=[f"dtt{st}"])
            S.op("act", f_act(dtt[:, st, :], dtt[:, st, :], AF.Ln, bias=1.0), r=[f"dtt{st}"], w=[f"dtt{st}"])
            S.op("dve", f_tt(lat[:, st, :], dtt[:, st, :], self.nA_bc[:], ALU.mult), r=[f"dtt{st}", "nA_bc"], w=[f"lat{st}"])
        for hh in range(2):
            wv, wk = self.w_next("in")
            for st in range(2):
                ps, pk = self.projT(wv, wk, 0, 512, st)
                S.op("act", f_act(ztok[:, st, hh * 512:(hh + 1) * 512], ps, AF.Silu), r=pk, w=[f"ztok{st}"])
        xck = [f"XC{c}" for c in range(8)]
        for st in range(2):
            ps, pk = self.ps_alloc(4)
            S.op("pe", f_trs([(ps[:, c8 * 128:(c8 + 1) * 128], XC[:, c8, st * 128:(st + 1) * 128], self.identf[:]) for c8 in range(8)]),
                 r=xck + ["identf"], w=pk)
            S.op("act", f_act(xtok[:, st, :], ps, AF.Copy), r=pk, w=[f"xtok{st}"])
            ps, pk = self.ps_alloc(1)
            psb = ps.bitcast(BF16)
            S.op("pe", f_trs([(psb[:, g * 128:(g + 1) * 128], BTb[:, g, st * 128:(st + 1) * 128], self.identb[:]) for g in range(4)]),
                 r=["BTb", "identb"], w=pk)
            S.op("act", f_act(Btok[:, st, :], psb, AF.Copy), r=pk, w=[f"Btok{st}"])
            S.op("dve", f_tt(xdtb[:, st, :].rearrange("p (h q) -> p h q", h=16), xtok[:, st, :].rearrange("p (h q) -> p h q", h=16),
                             dtt[:, st, :].rearrange("p (h o) -> p h o", o=1).broadcast_to([128, 16, 64]), ALU.mult),
                 r=[f"xtok{st}", f"dtt{st}"], w=[f"xdtb{st}"])
        S.barrier()
        self.dump("xtok", xtok, ["xtok0", "xtok1"], [128, 2, 1024])
        self.dump("dtt", dtt, ["dtt0", "dtt1"], [128, 2, 16])
        def c_s1(c):
            st, pb = c // 2, (c % 2) * 64
            P = slice(pb, pb + 64)
            cs = slice(c * 64, (c + 1) * 64)
            y = f"_{c % 2}"
            ebt, eblb, expD, Mb, cbm = ebt2[c % 2], eblb2[c % 2], expD2[c % 2], Mb2[c % 2], cbm2[c % 2]
            ps1, pk1 = self.ps_alloc(2)
            mms = [(ps1[P, g * 64:(g + 1) * 64], BTb[:, g, cs], CTb[:, g, cs], True, True) for g in range(4)]
            S.op("pe", f_mms(mms), r=["BTb", "CTb"], w=pk1)
            S.op("pe", f_mms([(ps1[:, 256:272], self.U2[P, :], lat[P, st, :], True, True),
                              (ps1[:, 272:288], self.ones[P, :], lat[P, st, :], True, True)]), r=["U2", "ones", f"lat{st}"], w=pk1)
            U4 = self.U2[P, 0:64].rearrange("p (o t) -> p o t", o=1)
            S.op("dve", f_tt(cbm[P, :].rearrange("p (g t) -> p g t", g=4), ps1[P, 0:256].rearrange("p (g t) -> p g t", g=4),
                             U4.broadcast_to([64, 4, 64]), ALU.mult), r=pk1 + ["U2"], w=["cbm" + y])
            S.op("act", f_act(ebt[P, :], ps1[P, 256:272], AF.Exp), r=pk1, w=["ebt" + y])
            S.op("act", f_act(eblb[:, :], ps1[:, 272:288], AF.Exp), r=pk1, w=["eblb" + y])
            S.op("dve", f_tt(Lmat[P, :].rearrange("p (h t) -> p h t", h=16), U4.broadcast_to([64, 16, 64]),
                             lat[P, st, :].rearrange("p (h o) -> p h o", o=1).broadcast_to([64, 16, 64]), ALU.mult), r=["U2", f"lat{st}"], w=["Lmat"])
            ps2, pk2 = self.ps_alloc(4)
            S.op("pe", f_mms([(ps2[:, 0:512], self.G2[P, :], Lmat[P, 0:512], True, True),
                              (ps2[:, 512:1024], self.G2[P, :], Lmat[P, 512:1024], True, True)]), r=["G2", "Lmat"], w=pk2)
            S.op("act", f_act(expD[P, :], ps2[P, :], AF.Exp), r=pk2, w=["expD" + y])
            S.op("dve", f_tt(Mb[P, :].rearrange("p (g h t) -> p g h t", g=4, h=4), expD[P, :].rearrange("p (g h t) -> p g h t", g=4, h=4),
                             cbm[P, :].rearrange("p (g o t) -> p g o t", g=4, o=1).broadcast_to([64, 4, 4, 64]), ALU.mult), r=["expD" + y, "cbm" + y], w=["Mb" + y])

        def c_s2(c):
            st, pb = c // 2, (c % 2) * 64
            P = slice(pb, pb + 64)
            cs = slice(c * 64, (c + 1) * 64)
            y = f"_{c % 2}"
            ebt, eblb, expD, Mb, cbm = ebt2[c % 2], eblb2[c % 2], expD2[c % 2], Mb2[c % 2], cbm2[c % 2]
            if T["sample"]:
                seq = T["a"] + c
                S.op("sp", f_dma(sld, I["state_ssm"][l, seq].rearrange("(b p) n -> p b n", p=128)), w=["nt"], dma=True)
                ps, pk = self.ps_alloc(4)
                S.op("pe", f_trs([(ps[:, b8 * 128:(b8 + 1) * 128], sld[:, b8, :], self.identf[:]) for b8 in range(8)]), r=["nt", "identf"], w=pk)
                S.op("act", f_act(Ssm[:], ps, AF.Copy), r=pk, w=["Ssm"])
                S.op("act", f_act(Ssmb[:], Ssm[:], AF.Copy), r=["Ssm"], w=["Ssmb"])
            ps3, pk3 = self.ps_alloc(4)
            mms = [(ps3[P, h * 64:(h + 1) * 64], Mb[P, h * 64:(h + 1) * 64], xdtb[P, st, h * 64:(h + 1) * 64], True, True) for h in range(16)]
            S.op("pe", f_mms(mms), r=["Mb" + y, f"xdtb{st}"], w=pk3)
            ps4, pk4 = self.ps_alloc(4)
            mms = [(ps4[P, g * 256:(g + 1) * 256], CTb[:, g, cs], Ssmb[:, g * 256:(g + 1) * 256], True, True) for g in range(4)]
            S.op("pe", f_mms(mms), r=["CTb", "Ssmb"], w=pk4)
            S.op("dve", f_tt(t1[P, :].rearrange("p (h q) -> p h q", h=16), ps4[P, :].rearrange("p (h q) -> p h q", h=16),
                             ebt[P, :].rearrange("p (h o) -> p h o", o=1).broadcast_to([64, 16, 64]), ALU.mult), r=pk4 + ["ebt" + y], w=["t1"])
            S.op("dve", f_tt(ytok[P, st, :], t1[P, :], ps3[P, :], ALU.add), r=pk3 + ["t1"], w=[f"ytok{st}"])
            S.op("dve", f_tt(dtw[P, :], dtt[P, st, :], expD[P, :].rearrange("p (h t) -> p h t", h=16)[:, :, 63], ALU.mult),
                 r=[f"dtt{st}", "expD" + y], w=["dtw"])
            S.op("dve", f_tt(xdtw[P, :].rearrange("p (h q) -> p h q", h=16), xtok[P, st, :].rearrange("p (h q) -> p h q", h=16),
                             dtw[P, :].rearrange("p (h o) -> p h o", o=1).broadcast_to([64, 16, 64]), ALU.mult), r=[f"xtok{st}", "dtw"], w=["xdtw"])
            ps5, pk5 = self.ps_alloc(4)
            mms = [(ps5[:, g * 256:(g + 1) * 256], Btok[P, st, g * 128:(g + 1) * 128], xdtw[P, g * 256:(g + 1) * 256], True, True) for g in range(4)]
            S.op("pe", f_mms(mms), r=[f"Btok{st}", "xdtw"], w=pk5)
            S3 = Ssm[:].rearrange("p (h q) -> p h q", h=16)
            S.op("dve", f_tt(S3, S3, eblb[:, :].rearrange("p (h o) -> p h o", o=1).broadcast_to([128, 16, 64]), ALU.mult), r=["Ssm", "eblb" + y], w=["Ssm"])
            S.op("dve", f_tt(Ssm[:], Ssm[:], ps5, ALU.add), r=pk5 + ["Ssm"], w=["Ssm"])
            S.op("act", f_act(Ssmb[:], Ssm[:], AF.Copy), r=["Ssm"], w=["Ssmb"])
            if T["sample"] or (T["last"] and c == 3):
                dst = (O["ns_s"][l, T["a"] + c] if T["sample"] else O["ns_p"][l, T["a"]])
                ps, pk = self.ps_alloc(4)
                S.op("pe", f_trs([(ps[:, b8 * 128:(b8 + 1) * 128], Ssm[:, b8 * 128:(b8 + 1) * 128], self.identf[:]) for b8 in range(8)]),
                     r=["Ssm", "identf"], w=pk)
                S.op("act", f_act(nt, ps, AF.Copy), r=pk, w=["nt"])
                S.op("pool", f_dma(dst.rearrange("(b p) n -> p b n", p=128), sld), r=["nt"], dma=True)

        c_s1(0)
        for c in range(4):
            if c + 1 < 4:
                c_s1(c + 1)
            c_s2(c)
        self.dump("ytok", ytok, ["ytok0", "ytok1"], [128, 2, 1024])
        for st in range(2):
            n3 = nt.rearrange("p (h q) -> p h q", h=16)
            S.op("dve", f_tt(n3, xtok[:, st, :].rearrange("p (h q) -> p h q", h=16),
                             self.dskip_bc[:].rearrange("p (h o) -> p h o", o=1).broadcast_to([128, 16, 64]), ALU.mult), r=[f"xtok{st}", "dskip_bc"], w=["nt"])
            S.op("dve", f_tt(nt, nt, ytok[:, st, :], ALU.add), r=["nt", f"ytok{st}"], w=["nt"])
            S.op("dve", f_tt(nt, nt, ztok[:, st, :], ALU.mult), r=["nt", f"ztok{st}"], w=["nt"])
            S.op("dve", f_tt(t1, nt, nt, ALU.mult), r=["nt"], w=["t1"])
            S.op("dve", (lambda o, i: (lambda e: e.tensor_reduce(out=o, in_=i, axis=AX.X, op=ALU.add)))(ss4[:, 0:4], t1.rearrange("p (g q) -> p g q", g=4)),
                 r=["t1"], w=["ss4"])
            S.op("act", f_act(ss4[:, 4:8], ss4[:, 0:4], AF.Sqrt, scale=1.0 / 256, bias=self.epsb[:, 0:1]), r=["ss4", "epsb"], w=["ss4b"])
            S.op("dve", (lambda o: (lambda e: e.reciprocal(out=o, in_=o)))(ss4[:, 4:8]), r=["ss4b"], w=["ss4b"])
            n4 = nt.rearrange("p (g q) -> p g q", g=4)
            S.op("dve", f_tt(n4, n4, ss4[:, 4:8].rearrange("p (g o) -> p g o", o=1).broadcast_to([128, 4, 256]), ALU.mult), r=["nt", "ss4b"], w=["nt"])
            S.op("dve", f_tt(ycb[:, st, :], nt, self.ssmn_bc[:], ALU.mult), r=["nt", "ssmn_bc"], w=[f"ycb{st}"])
        self.tr_to_yT(ycb, ["ycb0", "ycb1"], 2)

    def phaseM(self):
        S, A, T = self.S, self.arena, self.T
        l = T["l"]
        S.barrier()
        A.reset()
        sig = [[A.alloc([256], F32) for _ in range(4)] for _ in range(3)]
        accm = [A.alloc([256], F32) for _ in range(4)]
        tmpm = [A.alloc([256], F32) for _ in range(2)]
        otk = A.alloc([2, 2048], F32)
        junk = A.alloc([2048], BF16)
        ssq = A.alloc([4], F32)
        npost = A.alloc([2048], F32)
        S.op("sp", f_dma(npost, self.I["norm_post"][l].partition_broadcast(128)), w=["npost_bc"], dma=True)
        for g in range(4):
            for bi in range(3):
                wv, wk = self.w_next("in")
                for c4 in range(4):
                    ps, pk = self.projF(wv, wk, c4 * 128, 128)
                    S.op("act", f_act(sig[bi][c4], ps, AF.Sigmoid), r=pk, w=[f"sig{bi}_{c4}"])
            for bi in range(3):
                wv, wk = self.w_next("br")
                for c4 in range(4):
                    ps, pk = self.ps_alloc(1)
                    mms = [(ps, wv[:, wc, c4 * 128:(c4 + 1) * 128], self.yT[bi][:, wc, :], wc == 0, wc == 7) for wc in range(8)]
                    S.op("pe", f_mms(mms), r=[wk, f"yT{bi}"], w=pk)
                    if bi == 0:
                        S.op("dve", f_tt(accm[c4], sig[0][c4], ps, ALU.mult), r=pk + [f"sig0_{c4}"], w=[f"accm{c4}"])
                    else:
                        tm = tmpm[c4 % 2]
                        S.op("dve", f_tt(tm, sig[bi][c4], ps, ALU.mult), r=pk + [f"sig{bi}_{c4}"], w=[f"tmpm{c4 % 2}"])
                        if bi == 1:
                            S.op("dve", f_tt(accm[c4], accm[c4], tm, ALU.add), r=[f"accm{c4}", f"tmpm{c4 % 2}"], w=[f"accm{c4}"])
                        else:
                            S.op("dve", f_tt(self.mergedT[:, g * 4 + c4, :], accm[c4], tm, ALU.add), r=[f"accm{c4}", f"tmpm{c4 % 2}"], w=[f"mT{g * 4 + c4}"])
        mk = [f"mT{i}" for i in range(16)]
        self.dump("mergedT", self.mergedT[:], mk, [128, 16, 256], BF16)
        for g in range(4):
            wv, wk = self.w_next("out")
            for st in range(2):
                ps, pk = self.ps_alloc(2)
                mms = [(ps, self.mergedT[:, fc, st * 128:(st + 1) * 128], wv[:, fc, :], fc == 0, fc == 15) for fc in range(16)]
                S.op("pe", f_mms(mms), r=[wk] + mk, w=pk)
                S.op("act", f_act(otk[:, st, g * 512:(g + 1) * 512], ps, AF.Copy), r=pk, w=[f"otk{st}"])
        for st in range(2):
            S.op("act", f_act(junk, otk[:, st, :], AF.Square, accum_out=ssq[:, st:st + 1]), r=[f"otk{st}"], w=["junk", f"ssq{st}"])
            S.op("act", f_act(ssq[:, 2 + st:3 + st], ssq[:, st:st + 1], AF.Sqrt, scale=1.0 / 2048, bias=self.epsb[:, 0:1]),
                 r=[f"ssq{st}", "epsb"], w=[f"rs{st}"])
            S.op("dve", (lambda o: (lambda e: e.reciprocal(out=o, in_=o)))(ssq[:, 2 + st:3 + st]), r=[f"rs{st}"], w=[f"rs{st}"])
            S.op("dve", f_stt(otk[:, st, :], otk[:, st, :], ssq[:, 2 + st:3 + st], npost, ALU.mult, ALU.mult),
                 r=[f"otk{st}", f"rs{st}", "npost_bc"], w=[f"otk{st}"])
            S.op("dve", f_tt(self.X[:, st, :], self.X[:, st, :], otk[:, st, :], ALU.add), r=[f"otk{st}", f"X{st}"], w=[f"X{st}"])


def make_consts(SEQ):
    p = np.arange(128)
    i64 = p % 64
    ident = np.eye(128, dtype=np.float32)
    U2 = (i64[:, None] <= i64[None, :]).astype(np.float32)
    G2 = (i64[:, None] > i64[None, :]).astype(np.float32)
    Pm = np.zeros((128, 128), np.float32)
    for m in range(128):
        i = m % 64
        if i < 8:
            Pm[m + 8, m] = 1.0
        elif i < 16:
            Pm[m - 8, m] = 1.0
    pos = np.concatenate([np.arange(SEQ), PAST_LEN + np.arange(64)]).astype(np.float32)
    inv = np.power(np.float32(500000.0), -np.arange(8, dtype=np.float32) / np.float32(8)).astype(np.float32)
    ang = (pos[:, None] * inv[None, :]).astype(np.float32)
    cos, sin = np.cos(ang).astype(np.float32), np.sin(ang).astype(np.float32)
    cosT = np.ones((128, pos.shape[0]), np.float32)
    sinT = np.zeros((128, pos.shape[0]), np.float32)
    for q in range(128):
        i = q % 64
        if i < 8:
            cosT[q] = cos[:, i]
            sinT[q] = -sin[:, i]
        elif i < 16:
            cosT[q] = cos[:, i - 8]
            sinT[q] = sin[:, i - 8]
    return dict(c_ident=ident, c_U2=U2, c_G2=G2, c_Pm=Pm, c_cos=cosT, c_sin=sinT)


_W_NAMES = ("norm_pre", "norm_post", "w_in", "hgrn_lb_logits", "hgrn_norm", "swa_sinks", "conv_w", "conv_b",
            "dt_bias", "a_log", "d_skip", "ssm_norm", "w_branch_a", "w_branch_b", "w_branch_c", "w_out")


def run(cfg, inputs, n_cores):
    NPS, SEQ, NSS, DEPTH = cfg["NPS"], cfg["SEQ"], cfg["NSS"], cfg["DEPTH"]
    b = Builder(cfg)
    nc = b.build()
    consts = make_consts(SEQ)
    f = lambda a: np.ascontiguousarray(a, dtype=np.float32)
    in_maps = []
    for c in range(n_cores):
        m = {}
        m["x_prompt"] = f(inputs["x_prompt"][c * NPS:(c + 1) * NPS]).reshape(NPS * SEQ, 2048)
        sl = slice(c * NSS, (c + 1) * NSS)
        m["x_sample"] = f(inputs["x_sample"][sl]).reshape(NSS * 64, 2048)
        m["cache_swa_k"] = f(inputs["cache_swa_k"][:, sl]).reshape(DEPTH, NSS, 128, 256)
        m["cache_swa_v"] = f(inputs["cache_swa_v"][:, sl]).reshape(DEPTH, NSS, 128, 256)
        m["state_hgrn"] = f(inputs["state_hgrn"][:, sl])
        m["state_ssm"] = f(inputs["state_ssm"][:, sl]).reshape(DEPTH, NSS, 1024, 128)
        m["state_conv"] = f(inputs["state_conv"][:, sl])
        for nm in _W_NAMES:
            m[nm] = f(inputs[nm])
        m.update(consts)
        in_maps.append(m)
    res = run_bass_kernel_spmd(nc, in_maps, core_ids=list(range(n_cores)))
    R = res.results
    cat = lambda nm, ax: np.concatenate([np.asarray(r[nm]) for r in R], axis=ax)
    B, DB = NPS * n_cores, NSS * n_cores
    outs = (
        cat("y_prompt", 0).reshape(B, SEQ, 2048),
        cat("y_sample", 0).reshape(DB, 64, 2048),
        cat("nk_p", 1).reshape(DEPTH, B, 128, 4, 64),
        cat("nv_p", 1).reshape(DEPTH, B, 128, 4, 64),
        cat("nh_p", 1).reshape(DEPTH, B, 8, 128, 128),
        cat("ns_p", 1).reshape(DEPTH, B, 16, 64, 128),
        cat("nc_p", 1).reshape(DEPTH, B, 3, 2048),
        cat("nk_s", 1).reshape(DEPTH, DB, 128, 4, 64),
        cat("nv_s", 1).reshape(DEPTH, DB, 128, 4, 64),
        cat("nh_s", 1).reshape(DEPTH, DB, 8, 128, 128),
        cat("ns_s", 1).reshape(DEPTH, DB, 16, 64, 128),
        cat("nc_s", 1).reshape(DEPTH, DB, 3, 2048),
    )
    outs = tuple(np.ascontiguousarray(o, dtype=np.float32) for o in outs)
    dbg = {k: [np.asarray(r[k]).astype(np.float32) for r in R] for k in R[0] if k.startswith("dbg_")}
    return outs, dbg


def kernel(**inputs):
    cfg = dict(NPS=2, SEQ=2048, NSS=4, DEPTH=2)
    outs, _ = run(cfg, inputs, 8)
    return outs
```

```python
import contextlib
import numpy as np
import concourse.bass as bass
import concourse.mybir as mybir
from concourse.bass_utils import run_bass_kernel_spmd

F32 = mybir.dt.float32
BF16 = mybir.dt.bfloat16
AF = mybir.ActivationFunctionType
ALU = mybir.AluOpType
AX = mybir.AxisListType


class StopBuild(Exception):
    pass


class Sched:
    ENGS = ("pe", "act", "dve", "pool", "sp")
    NDSEM = 6
    NDSEM_POOL = 2

    def __init__(self, nc, stack):
        self.nc = nc
        self.stack = stack
        self.sems = {}
        for e in self.ENGS:
            self.sems[e] = stack.enter_context(nc.semaphore("s_" + e))
        self.dsems = {}
        for e in ("sp", "pool", "act"):
            self.dsems[e] = [stack.enter_context(nc.semaphore(f"d_{e}{i}")) for i in range(self.NDSEM)]
        self.sem_by_id = {}
        self.seq = {e: 0 for e in self.ENGS}
        self.dcount = {e: 0 for e in self.dsems}
        self.dval = {e: [0] * self.NDSEM for e in self.dsems}
        self.waited = {e: {} for e in self.ENGS}
        self.last_w = {}
        self.readers = {}
        self.streams = {e: [] for e in self.ENGS}
        self.nops = 0
        self.oplog = []

    def _semobj(self, name):
        if name in self.sems:
            return self.sems[name]
        q, i = name
        return self.dsems[q][i]

    def _need(self, eng, toks, tok, kind):
        if tok is None:
            return
        semname, value, src = tok
        if src == eng and semname in self.sems:
            if eng == "pe":
                return
        if toks.get(semname, 0) < value:
            toks[semname] = value

    stopn = None
    bg_vals = None

    def op(self, eng, fn, r=(), w=(), dma=False):
        if self.stopn is not None and self.nops >= self.stopn:
            raise StopBuild()
        toks = {}
        r = list(r)
        w = list(w) + [k for k in r if k.startswith("ps")]
        for k in r:
            self._need(eng, toks, self.last_w.get(k), "raw")
        for k in w:
            self._need(eng, toks, self.last_w.get(k), "waw")
            for sname, (val, src) in self.readers.get(k, {}).items():
                self._need(eng, toks, (sname, val, src), "war")
        if dma:
            q = eng
            i = self.dcount[q] % (self.NDSEM_POOL if q == "pool" else self.NDSEM)
            self.dcount[q] += 1
            semname = (q, i)
            prev = self.dval[q][i]
            if prev > 0 and toks.get(semname, 0) < prev:
                toks[semname] = prev
            self.dval[q][i] = prev + 16
            mytok = (semname, prev + 16, q)
            inc = (semname, 16)
        else:
            self.seq[eng] += 1
            mytok = (eng, self.seq[eng], eng)
            inc = (eng, 1)
        waits = []
        wd = self.waited[eng]
        for sname, val in toks.items():
            if wd.get(sname, 0) < val:
                wd[sname] = val
                waits.append((sname, val))
        self.streams[eng].append((waits, fn, inc))
        self.oplog.append((self.nops, eng, list(r), list(w), dma))
        for k in r:
            self.readers.setdefault(k, {})[mytok[0]] = (mytok[1], mytok[2])
        for k in w:
            self.last_w[k] = mytok
            self.readers[k] = {}
        self.nops += 1
        return mytok

    def final_wait(self, eng, toks):
        waits = []
        for (sname, val, _src) in toks:
            waits.append((sname, val))
        self.streams[eng].append((waits, None, None))

    def emit(self):
        nc = self.nc
        engobj = {"pe": "tensor", "act": "scalar", "dve": "vector", "pool": "gpsimd", "sp": "sync"}
        with nc.Block() as block:
            for e in self.ENGS:
                stream = self.streams[e]
                if not stream:
                    continue

                def body(eobj, stream=stream):
                    for waits, fn, inc in stream:
                        for sname, val in waits:
                            eobj.wait_ge(self._semobj(sname), val)
                        if fn is None:
                            continue
                        ins = fn(eobj)
                        ins.then_inc(self._semobj(inc[0]), inc[1])

                getattr(block, engobj[e])(body)

    def barrier(self):
        toks = []
        for e in self.ENGS:
            if self.seq[e] > 0:
                toks.append((e, self.seq[e]))
        for q in self.dsems:
            for i in range(self.NDSEM):
                if self.dval[q][i] > 0:
                    if self.bg_vals is not None and q == "pool" and self.dval[q][i] == self.bg_vals[i]:
                        continue
                    toks.append(((q, i), self.dval[q][i]))
        bgkeep_w = {k: v for k, v in self.last_w.items() if k.startswith("cv_")}
        for e in self.ENGS:
            waits = []
            wd = self.waited[e]
            for sname, val in toks:
                if sname == e and e == "pe":
                    continue
                if wd.get(sname, 0) < val:
                    wd[sname] = val
                    waits.append((sname, val))
            if waits:
                self.streams[e].append((waits, None, None))
        self.last_w = bgkeep_w
        self.readers = {}


class Arena:
    def __init__(self, nc, stack, name, nbytes):
        self.t = stack.enter_context(nc.sbuf_tensor(name, [128, nbytes // 4], F32))
        self.cap = nbytes // 4
        self.off = 0
        self.peak = 0

    def reset(self):
        self.peaks = getattr(self, "peaks", [])
        self.peaks.append(self.off * 4)
        self.off = 0

    def alloc_at(self, off, shape, dtype):
        save = self.off
        self.off = off
        ap = self.alloc(shape, dtype)
        self.off = max(save, self.off)
        return ap

    def alloc(self, shape, dtype):
        n = 1
        for s in shape:
            n *= s
        n4 = n if dtype == F32 else (n + 1) // 2
        assert self.off + n4 <= self.cap, f"arena overflow {self.off + n4} > {self.cap}"
        ap = self.t[:, self.off:self.off + n4]
        self.off += n4
        self.peak = max(self.peak, self.off)
        if dtype != F32:
            ap = ap.bitcast(dtype)
        if len(shape) == 2:
            ap = ap.rearrange("p (a b) -> p a b", a=shape[0])
        elif len(shape) == 3:
            ap = ap.rearrange("p (a b c) -> p a b c", a=shape[0], b=shape[1])
        return ap


def f_act(out, in_, func, **kw):
    return lambda e: e.activation(out=out, in_=in_, func=func, **kw)


def f_tt(out, in0, in1, op):
    return lambda e: e.tensor_tensor(out=out, in0=in0, in1=in1, op=op)


def f_ts(out, in0, s1, s2, op0, op1=None):
    if op1 is None:
        return lambda e: e.tensor_scalar(out=out, in0=in0, scalar1=s1, scalar2=None, op0=op0)
    return lambda e: e.tensor_scalar(out=out, in0=in0, scalar1=s1, scalar2=s2, op0=op0, op1=op1)


def f_stt(out, in0, scalar, in1, op0, op1):
    return lambda e: e.scalar_tensor_tensor(out=out, in0=in0, scalar=scalar, in1=in1, op0=op0, op1=op1)


def f_copy(out, in_):
    return lambda e: e.tensor_copy(out=out, in_=in_)


def f_dma(out, in_, **kw):
    return lambda e: e.dma_start(out=out, in_=in_, **kw)


def f_mms(mms):
    def fn(e):
        ins = None
        for (o, l, r, st, sp) in mms:
            ins = e.matmul(o, lhsT=l, rhs=r, start=st, stop=sp)
        return ins
    return fn


def f_trs(trs):
    def fn(e):
        ins = None
        for (o, i, idn) in trs:
            ins = e.transpose(o, i, idn)
        return ins
    return fn


D_MODEL = 2048
D_IN = 15888
TT = 256
COL = dict(a_q=0, a_f=1024, a_i=2048, a_g=3072, b_q=4096, b_k=5120, b_v=5376, b_g=5632,
           c_z=6656, c_x=7680, c_B=8704, c_C=9216, c_dt=9728, g_a=9744, g_b=11792, g_c=13840)
EPS = 1e-6
PAST_LEN = 4096
ATTN_SCALE = 64 ** -0.5


class Builder:
    def stop(self, name):
        if self.cfg.get("stop") == name:
            raise StopBuild()

    def __init__(self, cfg):
        self.cfg = cfg
        self.NPS, self.SEQ, self.NSS, self.DEPTH = cfg["NPS"], cfg["SEQ"], cfg["NSS"], cfg["DEPTH"]
        self.dbg = set(cfg.get("dbg", ()))
        self.dbg_out = {}
        self.NWB = cfg.get("NWB", 3)
        self.nc = bass.Bass("TRN2", target_bir_lowering=False)

    def din(self, name, shape, dt=F32):
        return self.nc.dram_tensor(name, list(shape), dt, kind="ExternalInput").ap()

    def dout(self, name, shape, dt=F32):
        return self.nc.dram_tensor(name, list(shape), dt, kind="ExternalOutput").ap()

    def dscr(self, name, shape, dt):
        return self.nc.dram_tensor(name, list(shape), dt, kind="Internal").ap()

    def sb(self, name, shape, dt):
        return self.stack.enter_context(self.nc.sbuf_tensor(name, list(shape), dt))

    def ps_alloc(self, nhalf):
        nb = 2 if nhalf > 2 else 1
        p = self.ps_ptr
        if p % nb:
            p += nb - p % nb
        if p + nb > 8:
            p = 0
        self.ps_ptr = (p + nb) % 8
        t = self.pp[p // 2]
        c0 = (p % 2) * 512
        return t[:, c0:c0 + 256 * nhalf], [f"ps{p + i}" for i in range(nb)]

    def dump(self, name, ap, keys, shape, dt=F32):
        if name not in self.dbg:
            return
        i = self.dbg_out.get(name, 0)
        self.dbg_out[name] = i + 1
        d = self.dout(f"dbg_{name}_{i}", shape, dt)
        self.S.op("pool", f_dma(d, ap), r=keys, dma=True)

    def wplan_layer(self, l):
        P = []
        for hh in range(2):
            P.append(("in", l, COL["a_f"] + hh * 512, 512))
            P.append(("in", l, COL["a_q"] + hh * 512, 512))
        for hh in range(2):
            P.append(("in", l, COL["a_i"] + hh * 512, 512))
        for hh in range(2):
            P.append(("in", l, COL["a_g"] + hh * 512, 512))
        for hh in range(2):
            P.append(("in", l, COL["b_q"] + hh * 512, 512))
        P.append(("bk", l, COL["b_k"], 256))
        P.append(("in", l, COL["b_v"], 256))
        for hh in range(2):
            P.append(("in", l, COL["b_g"] + hh * 512, 512))
        for hh in range(4):
            P.append(("in", l, COL["c_x"] + hh * 512, 512))
        P.append(("in", l, COL["c_dt"], 16))
        for hh in range(2):
            P.append(("in", l, COL["c_z"] + hh * 512, 512))
        for g in range(4):
            for nm in ("g_a", "g_b", "g_c"):
                P.append(("in", l, COL[nm] + g * 512, 512))
            for which in range(3):
                P.append(("br", l, which, g * 512))
        for g in range(4):
            P.append(("out", l, g * 512, 512))
        return P

    def w_issue(self, i):
        spec = self.wq[i]
        b = i % self.NWB
        buf = self.wbuf[b]
        S = self.S
        kind, l = spec[0], spec[1]
        if kind == "in":
            c0, n = spec[2], spec[3]
            src = self.wi_bf[l].rearrange("(kc p) e -> p kc e", p=128)[:, :, c0:c0 + n]
            dst = buf[:, 0:16 * n].rearrange("p (k c) -> p k c", k=16)
            deps = [f"cv_in{l}_{k}" for k in range(c0 // 1024, (c0 + n - 1) // 1024 + 1)]
            S.op("sp", f_dma(dst, src), r=deps, w=[f"W{b}"], dma=True)
        elif kind == "bk":
            c0 = spec[2]
            src = self.wi_bf[l].rearrange("(kc p) e -> p kc e", p=128)[:, :, c0:c0 + 256]
            dst = buf[:, 0:16 * 512].rearrange("p (k h u d) -> p k h u d", k=16, h=4, u=2)
            deps = [f"cv_in{l}_{k}" for k in range(c0 // 1024, (c0 + 255) // 1024 + 1)]
            for h in range(4):
                for u in range(2):
                    if self.cfg.get("exp3") and (h, u) != (0, 0):
                        continue
                    S.op("sp", f_dma(dst[:, :, h, u, :], src[:, :, h * 64:(h + 1) * 64]), r=deps, w=[f"W{b}"], dma=True)
        elif kind == "br":
            which, c0 = spec[2], spec[3]
            src = self.wb_bf[which][l].rearrange("(wc p) f -> p wc f", p=128)[:, :, c0:c0 + 512]
            dst = buf[:, 0:8 * 512].rearrange("p (k c) -> p k c", k=8)
            S.op("sp", f_dma(dst, src), r=[f"cv_br{which}_{l}"], w=[f"W{b}"], dma=True)
        elif kind == "out":
            c0 = spec[2]
            src = self.wo_bf[l].rearrange("(kc p) e -> p kc e", p=128)[:, :, c0:c0 + 512]
            dst = buf[:, 0:16 * 512].rearrange("p (k c) -> p k c", k=16)
            S.op("sp", f_dma(dst, src), r=[f"cv_out{l}"], w=[f"W{b}"], dma=True)

    def w_next(self, expect_kind, hold=0):
        i = self.w_cons
        self.w_cons += 1
        while self.w_iss < min(len(self.wq), i - hold + self.NWB):
            self.w_issue(self.w_iss)
            self.w_iss += 1
        spec = self.wq[i]
        assert spec[0] == expect_kind, (spec, expect_kind)
        b = i % self.NWB
        buf = self.wbuf[b]
        if spec[0] == "in":
            n = spec[3]
            v = buf[:, 0:16 * n].rearrange("p (k c) -> p k c", k=16)
        elif spec[0] == "bk":
            v = buf[:, 0:16 * 512].rearrange("p (k c) -> p k c", k=16)
        elif spec[0] == "br":
            v = buf[:, 0:8 * 512].rearrange("p (k c) -> p k c", k=8)
        else:
            v = buf[:, 0:16 * 512].rearrange("p (k c) -> p k c", k=16)
        return v, f"W{b}"

    def projF(self, wv, wkey, c0, m, ntok=TT, tok0=0):
        ps, pk = self.ps_alloc(1)
        mms = [(ps[0:m, 0:ntok], wv[:, kc, c0:c0 + m], self.hT[:, kc, tok0:tok0 + ntok], kc == 0, kc == 15) for kc in range(16)]
        self.S.op("pe", f_mms(mms), r=[wkey] + self.hTkeys, w=pk)
        return ps, pk

    def projT(self, wv, wkey, c0, n, st):
        ps, pk = self.ps_alloc(2 if n > 256 else 1)
        mms = [(ps[:, 0:n], self.hT[:, kc, st * 128:(st + 1) * 128], wv[:, kc, c0:c0 + n], kc == 0, kc == 15) for kc in range(16)]
        self.S.op("pe", f_mms(mms), r=[wkey] + self.hTkeys, w=pk)
        return ps, pk

    def build(self):
        nc = self.nc
        NPS, SEQ, NSS, DEPTH = self.NPS, self.SEQ, self.NSS, self.DEPTH
        with contextlib.ExitStack() as stack:
            self.stack = stack
            S = self.S = Sched(nc, stack)
            S.stopn = self.cfg.get("stopn")
            I = self.I = {}
            I["x_prompt"] = self.din("x_prompt", [NPS * SEQ, 2048])
            I["x_sample"] = self.din("x_sample", [NSS * 64, 2048])
            I["cache_k"] = self.din("cache_swa_k", [DEPTH, NSS, 128, 256])
            I["cache_v"] = self.din("cache_swa_v", [DEPTH, NSS, 128, 256])
            I["state_hgrn"] = self.din("state_hgrn", [DEPTH, NSS, 8, 128, 128])
            I["state_ssm"] = self.din("state_ssm", [DEPTH, NSS, 1024, 128])
            I["state_conv"] = self.din("state_conv", [DEPTH, NSS, 3, 2048])
            for nm, shp in (("norm_pre", [DEPTH, 2048]), ("norm_post", [DEPTH, 2048]), ("w_in", [DEPTH, 2048, D_IN]),
                            ("hgrn_lb_logits", [DEPTH, 1024]), ("hgrn_norm", [DEPTH, 128]), ("swa_sinks", [DEPTH, 16]),
                            ("conv_w", [DEPTH, 4, 2048]), ("conv_b", [DEPTH, 2048]), ("dt_bias", [DEPTH, 16]),
                            ("a_log", [DEPTH, 16]), ("d_skip", [DEPTH, 16]), ("ssm_norm", [DEPTH, 1024]),
                            ("w_branch_a", [DEPTH, 1024, 2048]), ("w_branch_b", [DEPTH, 1024, 2048]),
                            ("w_branch_c", [DEPTH, 1024, 2048]), ("w_out", [DEPTH, 2048, 2048]),
                            ("c_ident", [128, 128]), ("c_U2", [128, 128]), ("c_G2", [128, 128]), ("c_Pm", [128, 128]),
                            ("c_cos", [128, SEQ + 64]), ("c_sin", [128, SEQ + 64])):
                I[nm] = self.din(nm, shp)
            O = self.O = {}
            O["y_prompt"] = self.dout("y_prompt", [NPS * SEQ, 2048])
            O["y_sample"] = self.dout("y_sample", [NSS * 64, 2048])
            for sfx, nb in (("p", NPS), ("s", NSS)):
                O["nk_" + sfx] = self.dout("nk_" + sfx, [DEPTH, nb, 128, 256])
                O["nv_" + sfx] = self.dout("nv_" + sfx, [DEPTH, nb, 128, 256])
                O["nh_" + sfx] = self.dout("nh_" + sfx, [DEPTH, nb, 8, 128, 128])
                O["ns_" + sfx] = self.dout("ns_" + sfx, [DEPTH, nb, 1024, 128])
                O["nc_" + sfx] = self.dout("nc_" + sfx, [DEPTH, nb, 3, 2048])
            self.wi_bf = [self.dscr(f"wi_bf{l}", [2048, D_IN], BF16) for l in range(DEPTH)]
            self.wb_bf = [[self.dscr(f"wb_bf{w}_{l}", [1024, 2048], BF16) for l in range(DEPTH)] for w in range(3)]
            self.wo_bf = [self.dscr(f"wo_bf{l}", [2048, 2048], BF16) for l in range(DEPTH)]

            sb = self.sb
            self.X = sb("X", [128, 2, 2048], F32)
            self.hT = sb("hT", [128, 16, 256], BF16)
            self.hTkeys = [f"hT{k}" for k in range(16)]
            self.wbuf = [sb(f"wbuf{b}", [128, 8192], BF16) for b in range(self.NWB)]
            self.yT = [sb(f"yT{i}", [128, 8, 256], BF16) for i in range(3)]
            self.mergedT = sb("mergedT", [128, 16, 256], BF16)
            self.Sh = [sb(f"Sh{l}", [128, 8, 128], F32) for l in range(DEPTH)]
            shb = sb("Shb", [128, 8, 128], BF16)
            self.Shb = [shb for l in range(DEPTH)]
            self.Ssm = [sb(f"Ssm{l}", [128, 1024], F32) for l in range(DEPTH)]
            ssmb = sb("Ssmb", [128, 1024], BF16)
            self.Ssmb = [ssmb for l in range(DEPTH)]
            self.cctx = [sb(f"cctx{l}", [128, 16, 3], F32) for l in range(DEPTH)]
            self.KT = [sb(f"KT{l}", [128, 4, 384], BF16) for l in range(DEPTH)]
            self.Vt = [sb(f"Vt{l}", [128, 3, 256], BF16) for l in range(DEPTH)]
            self.ssmn_bc = sb("ssmn_bc", [128, 1024], F32)
            self.hgn_bc = sb("hgn_bc", [128, 128], F32)
            self.dskip_bc = sb("dskip_bc", [128, 16], F32)
            self.dtb_bc = sb("dtb_bc", [128, 16], F32)
            self.nA_bc = sb("nA_bc", [128, 16], F32)
            self.identf = sb("identf", [128, 128], F32)
            self.identb = sb("identb", [128, 128], BF16)
            self.U2 = sb("U2", [128, 128], F32)
            self.G2 = sb("G2", [128, 128], F32)
            self.Pm = sb("Pm", [128, 128], F32)
            self.Pmb = sb("Pmb", [128, 128], BF16)
            self.ones = sb("ones", [128, 128], F32)
            self.onesb = sb("onesb", [128, 128], BF16)
            self.cosT = sb("cosT", [128, 256], F32)
            self.sinT = sb("sinT", [128, 256], F32)
            self.npreT = sb("npreT", [128, DEPTH, 16], F32)
            self.lbT = sb("lbT", [128, DEPTH, 8], F32)
            self.omlbT = sb("omlbT", [128, DEPTH, 8], F32)
            self.elb = sb("elb", [128, DEPTH + 2, 8], F32)
            self.convw = sb("convw", [128, DEPTH, 16, 4], F32)
            self.convb = sb("convb", [128, DEPTH, 16], F32)
            self.esink = sb("esink", [128, DEPTH, 8], F32)
            self.epsb = sb("epsb", [128, 1], F32)
            self.pp = [stack.enter_context(nc.psum_tensor(f"pp{i}", [128, 1024], F32)) for i in range(4)]
            self.ps_ptr = 0
            self.arena = Arena(nc, stack, "arena", self.cfg.get("ARENA", 75 * 1024))

            self.setup()
            tiles = []
            for s in range(NPS):
                for j in range(SEQ // TT):
                    tiles.append(("p", s, j))
            for q0 in range(0, NSS, 4):
                tiles.append(("s", q0, 0))
            self.wq = []
            for _t in tiles:
                for l in range(DEPTH):
                    self.wq += self.wplan_layer(l)
            self.w_cons = 0
            self.w_iss = 0
            try:
                for ti, t in enumerate(tiles):
                    self.run_tile(t)
            except StopBuild:
                pass
            S.stopn = None
            for _i in range(self.cfg.get("delay", 0) or 0):
                de = self.cfg.get("delayeng", "dve")
                if de == "dve":
                    S.op("dve", f_copy(self.X[:, 1, :], self.X[:, 0, :]), r=["dlyX0"], w=["dlyX1"])
                elif de == "act":
                    S.op("act", f_act(self.X[:, 1, :], self.X[:, 0, :], AF.Copy), r=["dlyX0"], w=["dlyX1"])
                elif de == "actw":
                    S.op("act", f_act(self.X[:, 1, :], self.X[:, 0, :], AF.Copy), r=["X0", "hT0"], w=["dlyX1"])
                elif de == "pool":
                    S.op("pool", f_copy(self.X[:, 1, :], self.X[:, 0, :]), r=["dlyX0"], w=["dlyX1"])
            print("nops", S.nops, flush=True)
            fin = []
            for q in S.dsems:
                for i in range(S.NDSEM):
                    if S.dval[q][i] > 0:
                        fin.append(((q, i), S.dval[q][i], q))
            S.final_wait("sp", fin)
            S.emit()
        return nc

    def setup(self):
        S, I = self.S, self.I
        DEPTH = self.DEPTH
        for nm, t in (("c_ident", self.identf), ("c_U2", self.U2), ("c_G2", self.G2), ("c_Pm", self.Pm)):
            S.op("sp", f_dma(t[:], I[nm][:, :]), w=[t.name], dma=True)
        S.op("dve", f_copy(self.identb[:], self.identf[:]), r=["identf"], w=["identb"])
        S.op("dve", f_copy(self.Pmb[:], self.Pm[:]), r=["Pm"], w=["Pmb"])
        S.op("pool", lambda e: e.memset(self.ones[:], 1.0), w=["ones"])
        S.op("pool", lambda e: e.memset(self.onesb[:], 1.0), w=["onesb"])
        S.op("pool", lambda e: e.memset(self.epsb[:], EPS), w=["epsb"])
        with self.nc.allow_non_contiguous_dma(reason="tiny param loads"):
            for l in range(DEPTH):
                S.op("sp", f_dma(self.npreT[:, l, :], I["norm_pre"][l].rearrange("(kc p) -> p kc", p=128), allow_slow_non_contiguous=True), w=["npreT"], dma=True)
                S.op("sp", f_dma(self.elb[:, l, :], I["hgrn_lb_logits"][l].rearrange("(h k) -> k h", k=128), allow_slow_non_contiguous=True), w=["elb"], dma=True)
                for jj in range(4):
                    S.op("sp", f_dma(self.convw[:, l, :, jj], I["conv_w"][l, jj].rearrange("(cc p) -> p cc", p=128), allow_slow_non_contiguous=True), w=["convw"], dma=True)
                S.op("sp", f_dma(self.convb[:, l, :], I["conv_b"][l].rearrange("(cc p) -> p cc", p=128), allow_slow_non_contiguous=True), w=["convb"], dma=True)
                sv = I["swa_sinks"][l].rearrange("(cc u) -> u cc", u=2)
                for u in range(2):
                    S.op("sp", f_dma(self.esink[u * 64:(u + 1) * 64, l, :], sv[u].partition_broadcast(64), allow_slow_non_contiguous=True), w=["esink"], dma=True)
        S.op("act", f_act(self.esink[:], self.esink[:], AF.Exp), r=["esink"], w=["esink"])
        e = self.elb
        S.op("act", f_act(e[:, 0:DEPTH, :], e[:, 0:DEPTH, :], AF.Exp), r=["elb"], w=["elb"])
        tot, cum = e[:, DEPTH, :], e[:, DEPTH + 1, :]
        S.op("dve", f_copy(tot, e[:, 0, :]), r=["elb"], w=["elb"])
        for l in range(1, DEPTH):
            S.op("dve", f_tt(tot, tot, e[:, l, :], ALU.add), r=["elb"], w=["elb"])
        S.op("dve", lambda en: en.reciprocal(out=tot, in_=tot), r=["elb"], w=["elb"])
        S.op("dve", lambda en: en.memset(cum, 0.0), r=["elb"], w=["elb"])
        S.op("dve", lambda en: en.memset(self.lbT[:, 0, :], 0.0), w=["lb"])
        for l in range(1, DEPTH):
            S.op("dve", f_tt(cum, cum, e[:, l, :], ALU.add), r=["elb"], w=["elb"])
            S.op("dve", f_tt(self.lbT[:, l, :], cum, tot, ALU.mult), r=["elb"], w=["lb"])
        S.op("dve", f_ts(self.omlbT[:], self.lbT[:], -1.0, 1.0, ALU.mult, ALU.add), r=["lb"], w=["lb"])
        for l in range(DEPTH):
            nblk = (D_IN + 1023) // 1024
            for k in range(nblk):
                c0, c1 = k * 1024, min(D_IN, (k + 1) * 1024)
                S.op("pool", f_dma(self.wi_bf[l][:, c0:c1], I["w_in"][l][:, c0:c1]), w=[f"cv_in{l}_{k}"], dma=True)
            for w, nm in enumerate(("w_branch_a", "w_branch_b", "w_branch_c")):
                S.op("pool", f_dma(self.wb_bf[w][l][:, :], I[nm][l][:, :]), w=[f"cv_br{w}_{l}"], dma=True)
            S.op("pool", f_dma(self.wo_bf[l][:, :], I["w_out"][l][:, :]), w=[f"cv_out{l}"], dma=True)
        S.bg_vals = list(S.dval["pool"])

    def load_LC(self, l):
        S, I = self.S, self.I
        for t, nm in ((self.ssmn_bc, "ssm_norm"), (self.hgn_bc, "hgrn_norm"),
                      (self.dskip_bc, "d_skip"), (self.dtb_bc, "dt_bias"), (self.nA_bc, "a_log")):
            S.op("sp", f_dma(t[:], I[nm][l].partition_broadcast(128)), w=[t.name], dma=True)
        S.op("act", f_act(self.nA_bc[:], self.nA_bc[:], AF.Exp), r=["nA_bc"], w=["nA_bc"])
        S.op("dve", f_ts(self.nA_bc[:], self.nA_bc[:], -1.0, None, ALU.mult), r=["nA_bc"], w=["nA_bc"])

    def run_tile(self, t):
        S, I, O = self.S, self.I, self.O
        kind, a, j = t
        SEQ = self.SEQ
        if kind == "p":
            row0 = a * SEQ + j * TT
            xsrc, ydst = I["x_prompt"], O["y_prompt"]
            pos0 = j * TT
        else:
            row0 = a * 64
            xsrc, ydst = I["x_sample"], O["y_sample"]
        S.barrier()
        for st in range(2):
            S.op("sp", f_dma(self.X[:, st, :], xsrc[row0 + st * 128: row0 + (st + 1) * 128, :]), w=[f"X{st}"], dma=True)
        if kind == "p":
            S.op("sp", f_dma(self.cosT[:], I["c_cos"][:, pos0:pos0 + TT]), w=["cosT"], dma=True)
            S.op("sp", f_dma(self.sinT[:], I["c_sin"][:, pos0:pos0 + TT]), w=["sinT"], dma=True)
        else:
            for q in range(4):
                S.op("sp", f_dma(self.cosT[:, q * 64:(q + 1) * 64], I["c_cos"][:, SEQ:SEQ + 64]), w=["cosT"], dma=True)
                S.op("sp", f_dma(self.sinT[:, q * 64:(q + 1) * 64], I["c_sin"][:, SEQ:SEQ + 64]), w=["sinT"], dma=True)
        for l in range(self.DEPTH):
            self.T = dict(kind=kind, a=a, j=j, l=l, first=(kind == "p" and j == 0),
                          last=(kind == "p" and j == SEQ // TT - 1), sample=(kind == "s"))
            self.stop("setup")
            self.phase0()
            self.stop("p0")
            self.phaseA()
            self.stop("A")
            self.phaseB()
            self.stop("B")
            self.phaseC()
            self.stop("C")
            self.phaseM()
            self.stop("M")
        for st in range(2):
            S.op("pool", f_dma(ydst[row0 + st * 128: row0 + (st + 1) * 128, :], self.X[:, st, :]), r=[f"X{st}"], dma=True)

    def phase0(self):
        S, A, l = self.S, self.arena, self.T["l"]
        S.barrier()
        A.reset()
        if self.DEPTH > 1 or (self.T["kind"] == "p" and self.T["a"] == 0 and self.T["j"] == 0):
            self.load_LC(l)
        xn = A.alloc([2, 2048], BF16)
        junk = A.alloc([2048], BF16)
        ssq = A.alloc([4], F32)
        for st in range(2):
            S.op("act", f_act(junk, self.X[:, st, :], AF.Square, accum_out=ssq[:, st:st + 1]), r=[f"X{st}"], w=["junk", f"ssq{st}"])
            S.op("act", f_act(ssq[:, 2 + st:3 + st], ssq[:, st:st + 1], AF.Sqrt, scale=1.0 / 2048, bias=self.epsb[:, 0:1]),
                 r=[f"ssq{st}", "epsb"], w=[f"rs{st}"])
            S.op("dve", (lambda o: (lambda e: e.reciprocal(out=o, in_=o)))(ssq[:, 2 + st:3 + st]), r=[f"rs{st}"], w=[f"rs{st}"])
            S.op("act", f_act(xn[:, st, :], self.X[:, st, :], AF.Copy, scale=ssq[:, 2 + st:3 + st]), r=[f"X{st}", f"rs{st}"], w=[f"xn{st}"])
        for kc in range(16):
            ps, pk = self.ps_alloc(1)
            psb = ps.bitcast(BF16)
            trs = [(psb[:, st * 128:(st + 1) * 128], xn[:, st, kc * 128:(kc + 1) * 128], self.identb[:]) for st in range(2)]
            S.op("pe", f_trs(trs), r=["xn0", "xn1", "identb"], w=pk)
            if kc % 2 == 0:
                S.op("dve", f_ts(self.hT[:, kc, :], psb[:, 0:256], self.npreT[:, l, kc:kc + 1], None, ALU.mult), r=pk + ["npreT"], w=[f"hT{kc}"])
            else:
                S.op("act", f_act(self.hT[:, kc, :], psb[:, 0:256], AF.Copy, scale=self.npreT[:, l, kc:kc + 1]), r=pk + ["npreT"], w=[f"hT{kc}"])
        self.dump("hT", self.hT[:], self.hTkeys, [128, 16, 256], BF16)

    def phaseA(self):
        S, A, T = self.S, self.arena, self.T
        l = T["l"]
        S.barrier()
        A.reset()
        qT = A.alloc([8, 256], BF16)
        kT = A.alloc([8, 256], BF16)
        ebl = A.alloc([8, 4], F32)
        tmp = [{n: A.alloc([256], F32) for n in ("f", "k", "lf", "b", "eb", "enb", "sq")} for _ in range(2)]
        vtok = A.alloc([2, 1024], BF16)
        gtok = A.alloc([2, 1024], F32)
        otok = A.alloc([2, 1024], F32)
        ktok = [A.alloc([1024], BF16) for _ in range(2)]
        ATs = [A.alloc([512], BF16) for _ in range(2)]
        nt1 = A.alloc([1024], F32)
        ssh = A.alloc([16], F32)
        ya = A.alloc([2, 1024], BF16)
        Sh, Shb = self.Sh[l], self.Shb[l]
        if T["first"]:
            S.op("dve", lambda e: e.memset(Sh[:], 0.0), w=["Sh"])
            S.op("pool", lambda e: e.memset(Shb[:], 0.0), w=["Shb"])
        elif not T["sample"]:
            S.op("act", f_act(Shb[:], Sh[:], AF.Copy), r=["Sh"], w=["Shb"])
        qk = [f"qT{h}" for h in range(8)]
        kk = [f"kT{h}" for h in range(8)]
        for hh in range(2):
            wf, wfk = self.w_next("in")
            wq_, wqk = self.w_next("in", hold=1)
            for h4 in range(4):
                hd = hh * 4 + h4
                t = tmp[hd % 2]
                x = f"_{hd % 2}"
                ps, pk = self.projF(wf, wfk, h4 * 128, 128)
                S.op("act", f_act(t["f"], ps, AF.Sigmoid), r=pk, w=["tf" + x])
                S.op("dve", f_ts(t["f"], t["f"], self.omlbT[:, l, hd:hd + 1], self.lbT[:, l, hd:hd + 1], ALU.mult, ALU.add),
                     r=["tf" + x, "lb"], w=["tf" + x])
                S.op("dve", f_ts(t["k"], t["f"], -1.0, 1.0, ALU.mult, ALU.add), r=["tf" + x], w=["tk" + x])
                S.op("act", f_act(t["lf"], t["f"], AF.Ln), r=["tf" + x], w=["tlf" + x])
                for c in range(4):
                    cs = slice(c * 64, (c + 1) * 64)
                    S.op("dve", (lambda o, d1: (lambda e: e.tensor_tensor_scan(out=o, data0=self.ones[:, 0:64], data1=d1, initial=0.0,
                                                                                op0=ALU.mult, op1=ALU.add)))(t["b"][:, cs], t["lf"][:, cs]),
                         r=["tlf" + x, "ones"], w=["tb" + x])
                S.op("act", f_act(t["eb"], t["b"], AF.Exp), r=["tb" + x], w=["teb" + x])
                S.op("act", f_act(t["enb"], t["b"], AF.Exp, scale=-1.0), r=["tb" + x], w=["tenb" + x])
                if not self.cfg.get("exp2"):
                  S.op("act", f_act(ebl[:, hd, :], (t["eb"][:, 0:4] if self.cfg.get("exp1") else t["eb"].rearrange("p (c s) -> p c s", s=64)[:, :, 63]), AF.Copy), r=["teb" + x], w=["ebl"])
                psq, pkq = self.projF(wq_, wqk, h4 * 128, 128)
                S.op("act", f_act(t["sq"], psq, AF.Silu), r=pkq, w=["tsq" + x])
                if hd == 0:
                    for nm in ("f", "k", "lf", "b", "eb", "enb", "sq"):
                        self.dump("t_" + nm, t[nm], ["tf" + x, "tk" + x, "tlf" + x, "tb" + x, "teb" + x, "tenb" + x, "tsq" + x], [128, 256])
                S.op("dve", f_tt(qT[:, hd, :], t["sq"], t["eb"], ALU.mult), r=["tsq" + x, "teb" + x], w=[qk[hd]])
                S.op("dve", f_tt(kT[:, hd, :], t["k"], t["enb"], ALU.mult), r=["tk" + x, "tenb" + x], w=[kk[hd]])
        self.stop("A1")
        for hh in range(2):
            wv, wk = self.w_next("in")
            for st in range(2):
                ps, pk = self.projT(wv, wk, 0, 512, st)
                S.op("act", f_act(vtok[:, st, hh * 512:(hh + 1) * 512], ps, AF.Copy), r=pk, w=[f"vtok{st}"])
        for hh in range(2):
            wv, wk = self.w_next("in")
            for st in range(2):
                ps, pk = self.projT(wv, wk, 0, 512, st)
                S.op("act", f_act(gtok[:, st, hh * 512:(hh + 1) * 512], ps, AF.Silu), r=pk, w=[f"gtok{st}"])
        self.dump("qT", qT, qk, [128, 8, 256], BF16)
        self.dump("kT", kT, kk, [128, 8, 256], BF16)
        self.stop("A2")
        def a_s1(c):
            st, pb = c // 2, (c % 2) * 64
            P = slice(pb, pb + 64)
            cs = slice(c * 64, (c + 1) * 64)
            kt = ktok[c % 2]
            at = ATs[c % 2]
            ps, pk = self.ps_alloc(2)
            psb = ps.bitcast(BF16)
            trs = [(psb[P, hd * 128:(hd + 1) * 128], kT[:, hd, cs], self.identb[:]) for hd in range(8)]
            S.op("pe", f_trs(trs), r=kk + ["identb"], w=pk)
            S.op("act", f_act(kt[P, :], psb[P, :], AF.Copy), r=pk, w=[f"ktok{c % 2}"])
            ps2, pk2 = self.ps_alloc(2)
            mms = [(ps2[P, hd * 64:(hd + 1) * 64], kT[:, hd, cs], qT[:, hd, cs], True, True) for hd in range(8)]
            S.op("pe", f_mms(mms), r=kk + qk, w=pk2)
            S.op("dve", f_tt(at[P, :].rearrange("p (h t) -> p h t", h=8), ps2[P, :].rearrange("p (h t) -> p h t", h=8),
                             self.U2[P, 0:64].rearrange("p (o t) -> p o t", o=1).broadcast_to([64, 8, 64]), ALU.mult),
                 r=pk2 + ["U2"], w=[f"AT{c % 2}"])

        def a_s2(c):
            st, pb = c // 2, (c % 2) * 64
            P = slice(pb, pb + 64)
            cs = slice(c * 64, (c + 1) * 64)
            kt = ktok[c % 2]
            at = ATs[c % 2]
            if T["sample"]:
                seq = T["a"] + c
                S.op("sp", f_dma(Sh[:], self.I["state_hgrn"][l, seq].rearrange("h k v -> k h v")), w=["Sh"], dma=True)
                S.op("act", f_act(Shb[:], Sh[:], AF.Copy), r=["Sh"], w=["Shb"])
            ps3, pk3 = self.ps_alloc(4)
            mms = []
            for hd in range(8):
                o_ap = ps3[P, hd * 128:(hd + 1) * 128]
                mms.append((o_ap, at[P, hd * 64:(hd + 1) * 64], vtok[P, st, hd * 128:(hd + 1) * 128], True, False))
                mms.append((o_ap, qT[:, hd, cs], Shb[:, hd, :], False, True))
            S.op("pe", f_mms(mms), r=[f"AT{c % 2}", f"vtok{st}", "Shb"] + qk, w=pk3)
            S.op("act", f_act(otok[P, st, :], ps3[P, :], AF.Copy), r=pk3, w=[f"otok{st}"])
            ps4, pk4 = self.ps_alloc(4)
            mms = [(ps4[:, hd * 128:(hd + 1) * 128], kt[P, hd * 128:(hd + 1) * 128], vtok[P, st, hd * 128:(hd + 1) * 128], True, True)
                   for hd in range(8)]
            S.op("pe", f_mms(mms), r=[f"ktok{c % 2}", f"vtok{st}"], w=pk4)
            Sf = Sh[:].rearrange("p h v -> p (h v)")
            S.op("dve", f_tt(Sf, ps4, Sf, ALU.add), r=pk4 + ["Sh"], w=["Sh"])
            S.op("dve", f_tt(Sh[:], Sh[:], ebl[:, :, c:c + 1].broadcast_to([128, 8, 128]), ALU.mult), r=["Sh", "ebl"], w=["Sh"])
            S.op("act", f_act(Shb[:], Sh[:], AF.Copy), r=["Sh"], w=["Shb"])
            if T["sample"] or (T["last"] and c == 3):
                dst = (self.O["nh_s"][l, T["a"] + c] if T["sample"] else self.O["nh_p"][l, T["a"]])
                S.op("pool", f_dma(dst.rearrange("h k v -> k h v"), Sh[:]), r=["Sh"], dma=True)

        a_s1(0)
        for c in range(4):
            if c + 1 < 4:
                a_s1(c + 1)
            a_s2(c)
        self.dump("otok", otok, ["otok0", "otok1"], [128, 2, 1024])
        for st in range(2):
            o3 = otok[:, st, :]
            n3 = nt1.rearrange("p (h v) -> p h v", h=8)
            S.op("dve", f_tt(nt1, o3, o3, ALU.mult), r=[f"otok{st}"], w=["nt1"])
            S.op("dve", (lambda o, i: (lambda e: e.tensor_reduce(out=o, in_=i, axis=AX.X, op=ALU.add)))(ssh[:, 0:8], n3), r=["nt1"], w=["ssh"])
            S.op("act", f_act(ssh[:, 8:16], ssh[:, 0:8], AF.Sqrt, scale=1.0 / 128, bias=self.epsb[:, 0:1]), r=["ssh", "epsb"], w=["ssh2"])
            S.op("dve", (lambda o: (lambda e: e.reciprocal(out=o, in_=o)))(ssh[:, 8:16]), r=["ssh2"], w=["ssh2"])
            S.op("dve", f_tt(n3, o3.rearrange("p (h v) -> p h v", h=8),
                             ssh[:, 8:16].rearrange("p (h o) -> p h o", o=1).broadcast_to([128, 8, 128]), ALU.mult),
                 r=[f"otok{st}", "ssh2"], w=["nt1"])
            S.op("dve", f_tt(n3, n3, self.hgn_bc[:].rearrange("p (o v) -> p o v", o=1).broadcast_to([128, 8, 128]), ALU.mult),
                 r=["nt1", "hgn_bc"], w=["nt1"])
            S.op("dve", f_tt(ya[:, st, :], nt1, gtok[:, st, :], ALU.mult), r=["nt1", f"gtok{st}"], w=[f"ya{st}"])
        self.tr_to_yT(ya, ["ya0", "ya1"], 0)

    def tr_to_yT(self, y, ykeys, bi):
        S = self.S
        for wc in range(8):
            ps, pk = self.ps_alloc(1)
            psb = ps.bitcast(BF16)
            trs = [(psb[:, st * 128:(st + 1) * 128], y[:, st, wc * 128:(wc + 1) * 128], self.identb[:]) for st in range(2)]
            S.op("pe", f_trs(trs), r=ykeys + ["identb"], w=pk)
            if wc % 2 == 0:
                S.op("act", f_act(self.yT[bi][:, wc, :], psb[:, 0:256], AF.Copy), r=pk, w=[f"yT{bi}"])
            else:
                S.op("dve", f_copy(self.yT[bi][:, wc, :], psb[:, 0:256]), r=pk, w=[f"yT{bi}"])
        self.dump(f"yT{bi}", self.yT[bi][:], [f"yT{bi}"], [128, 8, 256], BF16)

    def rotary(self, ps, pk, t, x, out_ap, out_key):
        S = self.S
        S.op("act", f_act(t["x"], ps, AF.Copy), r=pk, w=["rx" + x])
        S.op("act", f_act(t["xb"], ps, AF.Copy), r=pk, w=["rxb" + x])
        ps2, pk2 = self.ps_alloc(1)
        S.op("pe", f_mms([(ps2, self.Pmb[:], t["xb"], True, True)]), r=["rxb" + x, "Pmb"], w=pk2)
        S.op("dve", f_tt(t["t1"], t["x"], self.cosT[:], ALU.mult), r=["rx" + x, "cosT"], w=["rt1" + x])
        S.op("dve", f_tt(t["t2"], ps2, self.sinT[:], ALU.mult), r=pk2 + ["sinT"], w=["rt2" + x])
        S.op("dve", f_tt(out_ap, t["t1"], t["t2"], ALU.add), r=["rt1" + x, "rt2" + x], w=[out_key])

    def phaseB(self):
        S, A, T, I, O = self.S, self.arena, self.T, self.I, self.O
        l = T["l"]
        S.barrier()
        A.reset()
        qr = A.alloc([8, 256], BF16)
        krf = A.alloc([4, 256], F32)
        sg = A.alloc([8, 256], F32)
        vf = A.alloc([2, 256], F32)
        xt = [{n: A.alloc([256], F32) for n in ("x", "t1", "t2")} for _ in range(2)]
        for _t in xt:
            _t["xb"] = A.alloc([256], BF16)
        pTs = [[A.alloc([1024], BF16) for _ in range(2)] for _ in range(2)]
        den = A.alloc([512], F32)
        ob = A.alloc([512], F32)
        ck = A.alloc([4, 2, 64], F32)
        cv = A.alloc([256], F32)
        kout = A.alloc([4, 128], F32)
        KT, Vt = self.KT[l], self.Vt[l]
        qrk = [f"qr{c}" for c in range(8)]
        for hh in range(2):
            wv, wk = self.w_next("in")
            for c4 in range(4):
                cc = hh * 4 + c4
                ps, pk = self.projF(wv, wk, c4 * 128, 128)
                self.rotary(ps, pk, xt[cc % 2], f"_{cc % 2}", qr[:, cc, :], qrk[cc])
        wv, wk = self.w_next("bk")
        for h in range(4):
            ps, pk = self.projF(wv, wk, h * 128, 128)
            self.rotary(ps, pk, xt[h % 2], f"_{h % 2}", krf[:, h, :], f"krf{h}")
            S.op("act", f_act(KT[:, h, 128:384], krf[:, h, :], AF.Copy), r=[f"krf{h}"], w=["KTcur"])
        wv, wk = self.w_next("in")
        for st in range(2):
            ps, pk = self.projT(wv, wk, 0, 256, st)
            S.op("act", f_act(vf[:, st, :], ps[:, 0:256], AF.Copy), r=pk, w=[f"vf{st}"])
            S.op("dve", f_copy(Vt[:, 1 + st, :], vf[:, st, :]), r=[f"vf{st}"], w=[f"Vt{1 + st}"])
        for hh in range(2):
            wv, wk = self.w_next("in")
            for c4 in range(4):
                cc = hh * 4 + c4
                ps, pk = self.projF(wv, wk, c4 * 128, 128)
                S.op("act", f_act(sg[:, cc, :], ps, AF.Silu), r=pk, w=["sg"])
        self.dump("qr", qr, qrk, [128, 8, 256], BF16)
        self.dump("krf", krf, [f"krf{h}" for h in range(4)], [128, 4, 256])
        seg_of = {}

        def b_s1(c):
            cs = slice(c * 64, (c + 1) * 64)
            pT = pTs[c % 2]
            if T["sample"]:
                seq = T["a"] + c
                ckv = I["cache_k"][l, seq].rearrange("k (h d) -> k h d", h=4)
                for u in range(2):
                    S.op("sp", f_dma(ck[:, :, u, :], ckv), w=["ck"], dma=True)
                S.op("sp", f_dma(cv, I["cache_v"][l, seq]), w=["cv"], dma=True)
                for h in range(4):
                    ps, pk = self.ps_alloc(1)
                    S.op("pe", f_trs([(ps[:, 0:128], ck[:, h, :, :].rearrange("p u d -> p (u d)"), self.identf[:])]), r=["ck", "identf"], w=pk)
                    S.op("act", f_act(KT[:, h, 0:128], ps[:, 0:128], AF.Copy), r=pk, w=["KTprev"])
                S.op("dve", f_copy(Vt[:, 0, :], cv), r=["cv"], w=["Vt0"])
                blocks = [(0, 0, 0), (0, 64, 64), (1 + c // 2, (c % 2) * 64, 128 + c * 64)]
            else:
                blocks = []
                for bl in (c - 2, c - 1, c):
                    if T["j"] * 4 + bl < 0:
                        continue
                    if bl < 0:
                        blocks.append((0, (bl + 2) * 64, (bl + 2) * 64))
                    else:
                        blocks.append((1 + bl // 2, (bl % 2) * 64, 128 + bl * 64))
            segs = []
            for b in blocks:
                if segs and segs[-1][0] == b[0] and segs[-1][1] == 0 and segs[-1][2] == 64 and b[1] == 64:
                    segs[-1] = (b[0], 0, 128, segs[-1][3])
                else:
                    segs.append((b[0], b[1], 64, b[2]))
            vkey = {0: "Vt0", 1: "Vt1", 2: "Vt2"}
            for si, (vidx, p0, nk, kc0) in enumerate(segs):
                ps, pk = self.ps_alloc(4)
                mms = []
                for h in range(4):
                    for u in range(2):
                        U = slice(u * 64, (u + 1) * 64)
                        g = u * 4 + h
                        mms.append((ps[p0:p0 + nk, g * 128:(g + 1) * 128], KT[U, h, kc0:kc0 + nk], qr[U, 2 * h:2 * h + 2, cs], True, True))
                S.op("pe", f_mms(mms), r=["KTcur", "KTprev"] + qrk, w=pk)
                S.op("act", f_act(pT[si][p0:p0 + nk, :], ps[p0:p0 + nk, :], AF.Exp, scale=ATTN_SCALE), r=pk, w=[f"pT{c % 2}_{si}"])
            seg_of[c] = segs

        def b_s2(c):
            cs = slice(c * 64, (c + 1) * 64)
            pT = pTs[c % 2]
            segs = seg_of[c]
            vkey = {0: "Vt0", 1: "Vt1", 2: "Vt2"}
            pso, pko = self.ps_alloc(2)
            psd, pkd = self.ps_alloc(2)
            mo, md = [], []
            ns = len(segs)
            for h in range(4):
                for u in range(2):
                    U = slice(u * 64, (u + 1) * 64)
                    g = u * 4 + h
                    for si, (vidx, p0, nk, kc0) in enumerate(segs):
                        KP = slice(p0, p0 + nk)
                        mo.append((pso[U, 2 * h * 64:(2 * h + 2) * 64], Vt[KP, vidx, h * 64:(h + 1) * 64], pT[si][KP, g * 128:(g + 1) * 128], si == 0, si == ns - 1))
                        md.append((psd[U, 2 * h * 64:(2 * h + 2) * 64], self.onesb[KP, 0:64], pT[si][KP, g * 128:(g + 1) * 128], si == 0, si == ns - 1))
            rk = [f"pT{c % 2}_{si}" for si in range(ns)] + [vkey[s[0]] for s in segs]
            S.op("pe", f_mms(mo), r=rk, w=pko)
            S.op("pe", f_mms(md), r=rk + ["onesb"], w=pkd)
            d3 = den.rearrange("p (c q) -> p c q", c=8)
            S.op("dve", f_tt(d3, psd.rearrange("p (c q) -> p c q", c=8),
                             self.esink[:, l, :].rearrange("p (c o) -> p c o", o=1).broadcast_to([128, 8, 64]), ALU.add), r=pkd + ["esink"], w=["den"])
            S.op("dve", (lambda o: (lambda e: e.reciprocal(out=o, in_=o)))(den), r=["den"], w=["den"])
            S.op("dve", f_tt(ob, pso, den, ALU.mult), r=pko + ["den"], w=["ob"])
            S.op("dve", f_tt(self.yT[1][:, :, cs], ob.rearrange("p (c q) -> p c q", c=8), sg[:, :, cs], ALU.mult), r=["ob", "sg"], w=["yT1"])
            if T["sample"]:
                seq = T["a"] + c
                pb = (c % 2) * 64
                S.op("pool", f_dma(O["nk_s"][l, seq, 0:64, :], I["cache_k"][l, seq, 64:128, :]), dma=True)
                S.op("pool", f_dma(O["nv_s"][l, seq, 0:64, :], I["cache_v"][l, seq, 64:128, :]), dma=True)
                ps, pk = self.ps_alloc(2)
                S.op("pe", f_trs([(ps[0:64, h * 128:(h + 1) * 128], krf[:, h, cs], self.identf[:]) for h in range(4)]),
                     r=[f"krf{h}" for h in range(4)] + ["identf"], w=pk)
                S.op("act", f_act(kout[0:64, :, 0:64], ps[0:64, :].rearrange("p (h x) -> p h x", h=4)[:, :, 0:64], AF.Copy), r=pk, w=["kout"])
                S.op("pool", f_dma(O["nk_s"][l, seq, 64:128, :].rearrange("k (h d) -> k h d", h=4), kout[0:64, :, 0:64]), r=["kout"], dma=True)
                S.op("pool", f_dma(O["nv_s"][l, seq, 64:128, :], vf[pb:pb + 64, c // 2, :]), r=[f"vf{c // 2}"], dma=True)

        if T["sample"]:
            for c in range(4):
                b_s1(c)
                b_s2(c)
        else:
            b_s1(0)
            for c in range(4):
                if c + 1 < 4:
                    b_s1(c + 1)
                b_s2(c)
        self.dump("yT1", self.yT[1][:], ["yT1"], [128, 8, 256], BF16)
        if T["last"]:
            s = T["a"]
            ps, pk = self.ps_alloc(2)
            S.op("pe", f_trs([(ps[:, h * 128:(h + 1) * 128], krf[:, h, 128:256], self.identf[:]) for h in range(4)]),
                 r=[f"krf{h}" for h in range(4)] + ["identf"], w=pk)
            S.op("act", f_act(kout[:, :, 0:64], ps.rearrange("p (h x) -> p h x", h=4)[:, :, 0:64], AF.Copy), r=pk, w=["kout"])
            S.op("pool", f_dma(O["nk_p"][l, s].rearrange("k (h d) -> k h d", h=4), kout[:, :, 0:64]), r=["kout"], dma=True)
            S.op("pool", f_dma(O["nv_p"][l, s], vf[:, 1, :]), r=["vf1"], dma=True)
        if not T["sample"]:
            S.op("act", f_act(KT[:, :, 0:128], KT[:, :, 256:384], AF.Copy), r=["KTcur"], w=["KTprev"])
            S.op("dve", f_copy(Vt[:, 0, :], Vt[:, 2, :]), r=["Vt2"], w=["Vt0"])

    def phaseC(self):
        S, A, T, I, O = self.S, self.arena, self.T, self.I, self.O
        l = T["l"]
        S.barrier()
        A.reset()
        XC = A.alloc([8, 256], F32)
        xr = [A.alloc([268], F32) for _ in range(2)]
        acc = [A.alloc([256], F32) for _ in range(2)]
        BTb = A.alloc([4, 256], BF16)
        CTb = A.alloc([4, 256], BF16)
        xtok = A.alloc([2, 1024], F32)
        xdtb = A.alloc([2, 1024], BF16)
        Btok = A.alloc([2, 512], BF16)
        ztok = A.alloc([2, 1024], F32)
        ytok = A.alloc_at(0, [2, 1024], F32)
        dtt = A.alloc([2, 16], F32)
        lat = A.alloc([2, 16], F32)
        dtw = A.alloc([16], F32)
        ebt2 = [A.alloc([16], F32) for _ in range(2)]
        eblb2 = [A.alloc([16], F32) for _ in range(2)]
        Lmat = A.alloc([1024], F32)
        expD2 = [A.alloc([1024], F32) for _ in range(2)]
        Mb2 = [A.alloc([1024], BF16) for _ in range(2)]
        cbm2 = [A.alloc([256], F32) for _ in range(2)]
        xdtw = A.alloc([1024], BF16)
        t1 = A.alloc([1024], F32)
        nt = A.alloc([1024], F32)
        ss4 = A.alloc([8], F32)
        ycb = A.alloc([2, 1024], BF16)
        sld = nt.rearrange("p (b n) -> p b n", b=8)
        scv = A.alloc([16, 4, 3], F32)
        Ssm, Ssmb, cctx = self.Ssm[l], self.Ssmb[l], self.cctx[l]
        if T["first"]:
            S.op("dve", lambda e: e.memset(Ssm[:], 0.0), w=["Ssm"])
            S.op("pool", lambda e: e.memset(Ssmb[:], 0.0), w=["Ssmb"])
            S.op("pool", lambda e: e.memset(cctx[:], 0.0), w=["cctx"])
        elif not T["sample"]:
            S.op("act", f_act(Ssmb[:], Ssm[:], AF.Copy), r=["Ssm"], w=["Ssmb"])
        if T["sample"]:
            with self.nc.allow_non_contiguous_dma(reason="conv state"):
                for q in range(4):
                    for jj in range(3):
                        S.op("sp", f_dma(scv[:, :, q, jj], I["state_conv"][l, T["a"] + q, jj].rearrange("(cc p) -> p cc", p=128), allow_slow_non_contiguous=True), w=["scv"], dma=True)
        if T["sample"]:
            segs = [(q * 64, q * 67, 64) for q in range(4)]
        else:
            segs = [(0, 0, 256)]
        for hh in range(4):
            wv, wk = self.w_next("in")
            for c4 in range(4):
                ch = hh * 4 + c4
                b = ch % 2
                x = f"_{b}"
                ps, pk = self.projF(wv, wk, c4 * 128, 128)
                if T["sample"]:
                    x3 = xr[b].rearrange("p (q s) -> p q s", s=67)
                    S.op("act", f_act(x3[:, :, 3:67], ps.rearrange("p (q t) -> p q t", t=64), AF.Copy), r=pk, w=["xr" + x])
                    S.op("act", f_act(x3[:, :, 0:3], scv[:, ch, :, :], AF.Copy), r=["scv"], w=["xrc" + x])
                else:
                    S.op("act", f_act(xr[b][:, 3:259], ps, AF.Copy), r=pk, w=["xr" + x])
                    S.op("act", f_act(xr[b][:, 0:3], cctx[:, ch, :], AF.Copy), r=["cctx"], w=["xrc" + x])
                for (o0, i0, n) in segs:
                    S.op("dve", f_ts(acc[b][:, o0:o0 + n], xr[b][:, i0:i0 + n], self.convw[:, l, ch, 0:1], None, ALU.mult),
                         r=["xr" + x, "xrc" + x, "convw"], w=["acc" + x])
                    for jj in range(1, 4):
                        S.op("dve", f_stt(acc[b][:, o0:o0 + n], xr[b][:, i0 + jj:i0 + jj + n], self.convw[:, l, ch, jj:jj + 1], acc[b][:, o0:o0 + n],
                                          ALU.mult, ALU.add), r=["xr" + x, "xrc" + x, "convw", "acc" + x], w=["acc" + x])
                if ch < 8:
                    dst, dk = XC[:, ch, :], f"XC{ch}"
                elif ch < 12:
                    dst, dk = BTb[:, ch - 8, :], "BTb"
                else:
                    dst, dk = CTb[:, ch - 12, :], "CTb"
                S.op("act", f_act(dst, acc[b], AF.Silu, bias=self.convb[:, l, ch:ch + 1]), r=["acc" + x, "convb"], w=[dk])
                with self.nc.allow_non_contiguous_dma(reason="conv state out"):
                    if T["sample"]:
                        x3 = xr[b].rearrange("p (q s) -> p q s", s=67)
                        for q in range(4):
                            S.op("pool", f_dma(O["nc_s"][l, T["a"] + q][:, ch * 128:(ch + 1) * 128].rearrange("j p -> p j"), x3[:, q, 64:67], allow_slow_non_contiguous=True),
                                 r=["xr" + x], dma=True)
                    else:
                        S.op("act", f_act(cctx[:, ch, :], xr[b][:, 256:259], AF.Copy), r=["xr" + x], w=["cctx"])
                        if T["last"]:
                            S.op("pool", f_dma(O["nc_p"][l, T["a"]][:, ch * 128:(ch + 1) * 128].rearrange("j p -> p j"), xr[b][:, 256:259], allow_slow_non_contiguous=True),
                                 r=["xr" + x], dma=True)
        wv, wk = self.w_next("in")
        for st in range(2):
            ps, pk = self.projT(wv, wk, 0, 16, st)
            S.op("dve", f_tt(dtt[:, st, :], ps[:, 0:16], self.dtb_bc[:], ALU.add), r=pk + ["dtb_bc"], w=[f"dtt{st}"])
            S.op("act", f_act(dtt[:, st, :], dtt[:, st, :], AF.Exp), r=[f"dtt{st}"], w=[f"dtt{st}"])
            S.op("act", f_act(dtt[:, st, :], dtt[:, st, :], AF.Ln, bias=1.0), r=[f"dtt{st}"], w=[f"dtt{st}"])
            S.op("dve", f_tt(lat[:, st, :], dtt[:, st, :], self.nA_bc[:], ALU.mult), r=[f"dtt{st}", "nA_bc"], w=[f"lat{st}"])
        for hh in range(2):
            wv, wk = self.w_next("in")
            for st in range(2):
                ps, pk = self.projT(wv, wk, 0, 512, st)
                S.op("act", f_act(ztok[:, st, hh * 512:(hh + 1) * 512], ps, AF.Silu), r=pk, w=[f"ztok{st}"])
        xck = [f"XC{c}" for c in range(8)]
        for st in range(2):
            ps, pk = self.ps_alloc(4)
            S.op("pe", f_trs([(ps[:, c8 * 128:(c8 + 1) * 128], XC[:, c8, st * 128:(st + 1) * 128], self.identf[:]) for c8 in range(8)]),
                 r=xck + ["identf"], w=pk)
            S.op("act", f_act(xtok[:, st, :], ps, AF.Copy), r=pk, w=[f"xtok{st}"])
            ps, pk = self.ps_alloc(1)
            psb = ps.bitcast(BF16)
            S.op("pe", f_trs([(psb[:, g * 128:(g + 1) * 128], BTb[:, g, st * 128:(st + 1) * 128], self.identb[:]) for g in range(4)]),
                 r=["BTb", "identb"], w=pk)
            S.op("act", f_act(Btok[:, st, :], psb, AF.Copy), r=pk, w=[f"Btok{st}"])
            S.op("dve", f_tt(xdtb[:, st, :].rearrange("p (h q) -> p h q", h=16), xtok[:, st, :].rearrange("p (h q) -> p h q", h=16),
                             dtt[:, st, :].rearrange("p (h o) -> p h o", o=1).broadcast_to([128, 16, 64]), ALU.mult),
                 r=[f"xtok{st}", f"dtt{st}"], w=[f"xdtb{st}"])
        S.barrier()
        self.dump("xtok", xtok, ["xtok0", "xtok1"], [128, 2, 1024])
        self.dump("dtt", dtt, ["dtt0", "dtt1"], [128, 2, 16])
        def c_s1(c):
            st, pb = c // 2, (c % 2) * 64
            P = slice(pb, pb + 64)
            cs = slice(c * 64, (c + 1) * 64)
            y = f"_{c % 2}"
            ebt, eblb, expD, Mb, cbm = ebt2[c % 2], eblb2[c % 2], expD2[c % 2], Mb2[c % 2], cbm2[c % 2]
            ps1, pk1 = self.ps_alloc(2)
            mms = [(ps1[P, g * 64:(g + 1) * 64], BTb[:, g, cs], CTb[:, g, cs], True, True) for g in range(4)]
            S.op("pe", f_mms(mms), r=["BTb", "CTb"], w=pk1)
            S.op("pe", f_mms([(ps1[:, 256:272], self.U2[P, :], lat[P, st, :], True, True),
                              (ps1[:, 272:288], self.ones[P, :], lat[P, st, :], True, True)]), r=["U2", "ones", f"lat{st}"], w=pk1)
            U4 = self.U2[P, 0:64].rearrange("p (o t) -> p o t", o=1)
            S.op("dve", f_tt(cbm[P, :].rearrange("p (g t) -> p g t", g=4), ps1[P, 0:256].rearrange("p (g t) -> p g t", g=4),
                             U4.broadcast_to([64, 4, 64]), ALU.mult), r=pk1 + ["U2"], w=["cbm" + y])
            S.op("act", f_act(ebt[P, :], ps1[P, 256:272], AF.Exp), r=pk1, w=["ebt" + y])
            S.op("act", f_act(eblb[:, :], ps1[:, 272:288], AF.Exp), r=pk1, w=["eblb" + y])
            S.op("dve", f_tt(Lmat[P, :].rearrange("p (h t) -> p h t", h=16), U4.broadcast_to([64, 16, 64]),
                             lat[P, st, :].rearrange("p (h o) -> p h o", o=1).broadcast_to([64, 16, 64]), ALU.mult), r=["U2", f"lat{st}"], w=["Lmat"])
            ps2, pk2 = self.ps_alloc(4)
            S.op("pe", f_mms([(ps2[:, 0:512], self.G2[P, :], Lmat[P, 0:512], True, True),
                              (ps2[:, 512:1024], self.G2[P, :], Lmat[P, 512:1024], True, True)]), r=["G2", "Lmat"], w=pk2)
            S.op("act", f_act(expD[P, :], ps2[P, :], AF.Exp), r=pk2, w=["expD" + y])
            S.op("dve", f_tt(Mb[P, :].rearrange("p (g h t) -> p g h t", g=4, h=4), expD[P, :].rearrange("p (g h t) -> p g h t", g=4, h=4),
                             cbm[P, :].rearrange("p (g o t) -> p g o t", g=4, o=1).broadcast_to([64, 4, 4, 64]), ALU.mult), r=["expD" + y, "cbm" + y], w=["Mb" + y])

        def c_s2(c):
            st, pb = c // 2, (c % 2) * 64
            P = slice(pb, pb + 64)
            cs = slice(c * 64, (c + 1) * 64)
            y = f"_{c % 2}"
            ebt, eblb, expD, Mb, cbm = ebt2[c % 2], eblb2[c % 2], expD2[c % 2], Mb2[c % 2], cbm2[c % 2]
            if T["sample"]:
                seq = T["a"] + c
                S.op("sp", f_dma(sld, I["state_ssm"][l, seq].rearrange("(b p) n -> p b n", p=128)), w=["nt"], dma=True)
                ps, pk = self.ps_alloc(4)
                S.op("pe", f_trs([(ps[:, b8 * 128:(b8 + 1) * 128], sld[:, b8, :], self.identf[:]) for b8 in range(8)]), r=["nt", "identf"], w=pk)
                S.op("act", f_act(Ssm[:], ps, AF.Copy), r=pk, w=["Ssm"])
                S.op("act", f_act(Ssmb[:], Ssm[:], AF.Copy), r=["Ssm"], w=["Ssmb"])
            ps3, pk3 = self.ps_alloc(4)
            mms = [(ps3[P, h * 64:(h + 1) * 64], Mb[P, h * 64:(h + 1) * 64], xdtb[P, st, h * 64:(h + 1) * 64], True, True) for h in range(16)]
            S.op("pe", f_mms(mms), r=["Mb" + y, f"xdtb{st}"], w=pk3)
            ps4, pk4 = self.ps_alloc(4)
            mms = [(ps4[P, g * 256:(g + 1) * 256], CTb[:, g, cs], Ssmb[:, g * 256:(g + 1) * 256], True, True) for g in range(4)]
            S.op("pe", f_mms(mms), r=["CTb", "Ssmb"], w=pk4)
            S.op("dve", f_tt(t1[P, :].rearrange("p (h q) -> p h q", h=16), ps4[P, :].rearrange("p (h q) -> p h q", h=16),
                             ebt[P, :].rearrange("p (h o) -> p h o", o=1).broadcast_to([64, 16, 64]), ALU.mult), r=pk4 + ["ebt" + y], w=["t1"])
            S.op("dve", f_tt(ytok[P, st, :], t1[P, :], ps3[P, :], ALU.add), r=pk3 + ["t1"], w=[f"ytok{st}"])
            S.op("dve", f_tt(dtw[P, :], dtt[P, st, :], expD[P, :].rearrange("p (h t) -> p h t", h=16)[:, :, 63], ALU.mult),
                 r=[f"dtt{st}", "expD" + y], w=["dtw"])
            S.op("dve", f_tt(xdtw[P, :].rearrange("p (h q) -> p h q", h=16), xtok[P, st, :].rearrange("p (h q) -> p h q", h=16),
                             dtw[P, :].rearrange("p (h o) -> p h o", o=1).broadcast_to([64, 16, 64]), ALU.mult), r=[f"xtok{st}", "dtw"], w=["xdtw"])
            ps5, pk5 = self.ps_alloc(4)
            mms = [(ps5[:, g * 256:(g + 1) * 256], Btok[P, st, g * 128:(g + 1) * 128], xdtw[P, g * 256:(g + 1) * 256], True, True) for g in range(4)]
            S.op("pe", f_mms(mms), r=[f"Btok{st}", "xdtw"], w=pk5)
            S3 = Ssm[:].rearrange("p (h q) -> p h q", h=16)
            S.op("dve", f_tt(S3, S3, eblb[:, :].rearrange("p (h o) -> p h o", o=1).broadcast_to([128, 16, 64]), ALU.mult), r=["Ssm", "eblb" + y], w=["Ssm"])
            S.op("dve", f_tt(Ssm[:], Ssm[:], ps5, ALU.add), r=pk5 + ["Ssm"], w=["Ssm"])
            S.op("act", f_act(Ssmb[:], Ssm[:], AF.Copy), r=["Ssm"], w=["Ssmb"])
            if T["sample"] or (T["last"] and c == 3):
                dst = (O["ns_s"][l, T["a"] + c] if T["sample"] else O["ns_p"][l, T["a"]])
                ps, pk = self.ps_alloc(4)
                S.op("pe", f_trs([(ps[:, b8 * 128:(b8 + 1) * 128], Ssm[:, b8 * 128:(b8 + 1) * 128], self.identf[:]) for b8 in range(8)]),
                     r=["Ssm", "identf"], w=pk)
                S.op("act", f_act(nt, ps, AF.Copy), r=pk, w=["nt"])
                S.op("pool", f_dma(dst.rearrange("(b p) n -> p b n", p=128), sld), r=["nt"], dma=True)

        c_s1(0)
        for c in range(4):
            if c + 1 < 4:
                c_s1(c + 1)
            c_s2(c)
        self.dump("ytok", ytok, ["ytok0", "ytok1"], [128, 2, 1024])
        for st in range(2):
            n3 = nt.rearrange("p (h q) -> p h q", h=16)
            S.op("dve", f_tt(n3, xtok[:, st, :].rearrange("p (h q) -> p h q", h=16),
                             self.dskip_bc[:].rearrange("p (h o) -> p h o", o=1).broadcast_to([128, 16, 64]), ALU.mult), r=[f"xtok{st}", "dskip_bc"], w=["nt"])
            S.op("dve", f_tt(nt, nt, ytok[:, st, :], ALU.add), r=["nt", f"ytok{st}"], w=["nt"])
            S.op("dve", f_tt(nt, nt, ztok[:, st, :], ALU.mult), r=["nt", f"ztok{st}"], w=["nt"])
            S.op("dve", f_tt(t1, nt, nt, ALU.mult), r=["nt"], w=["t1"])
            S.op("dve", (lambda o, i: (lambda e: e.tensor_reduce(out=o, in_=i, axis=AX.X, op=ALU.add)))(ss4[:, 0:4], t1.rearrange("p (g q) -> p g q", g=4)),
                 r=["t1"], w=["ss4"])
            S.op("act", f_act(ss4[:, 4:8], ss4[:, 0:4], AF.Sqrt, scale=1.0 / 256, bias=self.epsb[:, 0:1]), r=["ss4", "epsb"], w=["ss4b"])
            S.op("dve", (lambda o: (lambda e: e.reciprocal(out=o, in_=o)))(ss4[:, 4:8]), r=["ss4b"], w=["ss4b"])
            n4 = nt.rearrange("p (g q) -> p g q", g=4)
            S.op("dve", f_tt(n4, n4, ss4[:, 4:8].rearrange("p (g o) -> p g o", o=1).broadcast_to([128, 4, 256]), ALU.mult), r=["nt", "ss4b"], w=["nt"])
            S.op("dve", f_tt(ycb[:, st, :], nt, self.ssmn_bc[:], ALU.mult), r=["nt", "ssmn_bc"], w=[f"ycb{st}"])
        self.tr_to_yT(ycb, ["ycb0", "ycb1"], 2)

    def phaseM(self):
        S, A, T = self.S, self.arena, self.T
        l = T["l"]
        S.barrier()
        A.reset()
        sig = [[A.alloc([256], F32) for _ in range(4)] for _ in range(3)]
        accm = [A.alloc([256], F32) for _ in range(4)]
        tmpm = [A.alloc([256], F32) for _ in range(2)]
        otk = A.alloc([2, 2048], F32)
        junk = A.alloc([2048], BF16)
        ssq = A.alloc([4], F32)
        npost = A.alloc([2048], F32)
        S.op("sp", f_dma(npost, self.I["norm_post"][l].partition_broadcast(128)), w=["npost_bc"], dma=True)
        for g in range(4):
            for bi in range(3):
                wv, wk = self.w_next("in")
                for c4 in range(4):
                    ps, pk = self.projF(wv, wk, c4 * 128, 128)
                    S.op("act", f_act(sig[bi][c4], ps, AF.Sigmoid), r=pk, w=[f"sig{bi}_{c4}"])
            for bi in range(3):
                wv, wk = self.w_next("br")
                for c4 in range(4):
                    ps, pk = self.ps_alloc(1)
                    mms = [(ps, wv[:, wc, c4 * 128:(c4 + 1) * 128], self.yT[bi][:, wc, :], wc == 0, wc == 7) for wc in range(8)]
                    S.op("pe", f_mms(mms), r=[wk, f"yT{bi}"], w=pk)
                    if bi == 0:
                        S.op("dve", f_tt(accm[c4], sig[0][c4], ps, ALU.mult), r=pk + [f"sig0_{c4}"], w=[f"accm{c4}"])
                    else:
                        tm = tmpm[c4 % 2]
                        S.op("dve", f_tt(tm, sig[bi][c4], ps, ALU.mult), r=pk + [f"sig{bi}_{c4}"], w=[f"tmpm{c4 % 2}"])
                        if bi == 1:
                            S.op("dve", f_tt(accm[c4], accm[c4], tm, ALU.add), r=[f"accm{c4}", f"tmpm{c4 % 2}"], w=[f"accm{c4}"])
                        else:
                            S.op("dve", f_tt(self.mergedT[:, g * 4 + c4, :], accm[c4], tm, ALU.add), r=[f"accm{c4}", f"tmpm{c4 % 2}"], w=[f"mT{g * 4 + c4}"])
        mk = [f"mT{i}" for i in range(16)]
        self.dump("mergedT", self.mergedT[:], mk, [128, 16, 256], BF16)
        for g in range(4):
            wv, wk = self.w_next("out")
            for st in range(2):
                ps, pk = self.ps_alloc(2)
                mms = [(ps, self.mergedT[:, fc, st * 128:(st + 1) * 128], wv[:, fc, :], fc == 0, fc == 15) for fc in range(16)]
                S.op("pe", f_mms(mms), r=[wk] + mk, w=pk)
                S.op("act", f_act(otk[:, st, g * 512:(g + 1) * 512], ps, AF.Copy), r=pk, w=[f"otk{st}"])
        for st in range(2):
            S.op("act", f_act(junk, otk[:, st, :], AF.Square, accum_out=ssq[:, st:st + 1]), r=[f"otk{st}"], w=["junk", f"ssq{st}"])
            S.op("act", f_act(ssq[:, 2 + st:3 + st], ssq[:, st:st + 1], AF.Sqrt, scale=1.0 / 2048, bias=self.epsb[:, 0:1]),
                 r=[f"ssq{st}", "epsb"], w=[f"rs{st}"])
            S.op("dve", (lambda o: (lambda e: e.reciprocal(out=o, in_=o)))(ssq[:, 2 + st:3 + st]), r=[f"rs{st}"], w=[f"rs{st}"])
            S.op("dve", f_stt(otk[:, st, :], otk[:, st, :], ssq[:, 2 + st:3 + st], npost, ALU.mult, ALU.mult),
                 r=[f"otk{st}", f"rs{st}", "npost_bc"], w=[f"otk{st}"])
            S.op("dve", f_tt(self.X[:, st, :], self.X[:, st, :], otk[:, st, :], ALU.add), r=[f"otk{st}", f"X{st}"], w=[f"X{st}"])


def make_consts(SEQ):
    p = np.arange(128)
    i64 = p % 64
    ident = np.eye(128, dtype=np.float32)
    U2 = (i64[:, None] <= i64[None, :]).astype(np.float32)
    G2 = (i64[:, None] > i64[None, :]).astype(np.float32)
    Pm = np.zeros((128, 128), np.float32)
    for m in range(128):
        i = m % 64
        if i < 8:
            Pm[m + 8, m] = 1.0
        elif i < 16:
            Pm[m - 8, m] = 1.0
    pos = np.concatenate([np.arange(SEQ), PAST_LEN + np.arange(64)]).astype(np.float32)
    inv = np.power(np.float32(500000.0), -np.arange(8, dtype=np.float32) / np.float32(8)).astype(np.float32)
    ang = (pos[:, None] * inv[None, :]).astype(np.float32)
    cos, sin = np.cos(ang).astype(np.float32), np.sin(ang).astype(np.float32)
    cosT = np.ones((128, pos.shape[0]), np.float32)
    sinT = np.zeros((128, pos.shape[0]), np.float32)
    for q in range(128):
        i = q % 64
        if i < 8:
            cosT[q] = cos[:, i]
            sinT[q] = -sin[:, i]
        elif i < 16:
            cosT[q] = cos[:, i - 8]
            sinT[q] = sin[:, i - 8]
    return dict(c_ident=ident, c_U2=U2, c_G2=G2, c_Pm=Pm, c_cos=cosT, c_sin=sinT)


_W_NAMES = ("norm_pre", "norm_post", "w_in", "hgrn_lb_logits", "hgrn_norm", "swa_sinks", "conv_w", "conv_b",
            "dt_bias", "a_log", "d_skip", "ssm_norm", "w_branch_a", "w_branch_b", "w_branch_c", "w_out")


def run(cfg, inputs, n_cores):
    NPS, SEQ, NSS, DEPTH = cfg["NPS"], cfg["SEQ"], cfg["NSS"], cfg["DEPTH"]
    b = Builder(cfg)
    nc = b.build()
    consts = make_consts(SEQ)
    f = lambda a: np.ascontiguousarray(a, dtype=np.float32)
    in_maps = []
    for c in range(n_cores):
        m = {}
        m["x_prompt"] = f(inputs["x_prompt"][c * NPS:(c + 1) * NPS]).reshape(NPS * SEQ, 2048)
        sl = slice(c * NSS, (c + 1) * NSS)
        m["x_sample"] = f(inputs["x_sample"][sl]).reshape(NSS * 64, 2048)
        m["cache_swa_k"] = f(inputs["cache_swa_k"][:, sl]).reshape(DEPTH, NSS, 128, 256)
        m["cache_swa_v"] = f(inputs["cache_swa_v"][:, sl]).reshape(DEPTH, NSS, 128, 256)
        m["state_hgrn"] = f(inputs["state_hgrn"][:, sl])
        m["state_ssm"] = f(inputs["state_ssm"][:, sl]).reshape(DEPTH, NSS, 1024, 128)
        m["state_conv"] = f(inputs["state_conv"][:, sl])
        for nm in _W_NAMES:
            m[nm] = f(inputs[nm])
        m.update(consts)
        in_maps.append(m)
    res = run_bass_kernel_spmd(nc, in_maps, core_ids=list(range(n_cores)))
    R = res.results
    cat = lambda nm, ax: np.concatenate([np.asarray(r[nm]) for r in R], axis=ax)
    B, DB = NPS * n_cores, NSS * n_cores
    outs = (
        cat("y_prompt", 0).reshape(B, SEQ, 2048),
        cat("y_sample", 0).reshape(DB, 64, 2048),
        cat("nk_p", 1).reshape(DEPTH, B, 128, 4, 64),
        cat("nv_p", 1).reshape(DEPTH, B, 128, 4, 64),
        cat("nh_p", 1).reshape(DEPTH, B, 8, 128, 128),
        cat("ns_p", 1).reshape(DEPTH, B, 16, 64, 128),
        cat("nc_p", 1).reshape(DEPTH, B, 3, 2048),
        cat("nk_s", 1).reshape(DEPTH, DB, 128, 4, 64),
        cat("nv_s", 1).reshape(DEPTH, DB, 128, 4, 64),
        cat("nh_s", 1).reshape(DEPTH, DB, 8, 128, 128),
        cat("ns_s", 1).reshape(DEPTH, DB, 16, 64, 128),
        cat("nc_s", 1).reshape(DEPTH, DB, 3, 2048),
    )
    outs = tuple(np.ascontiguousarray(o, dtype=np.float32) for o in outs)
    dbg = {k: [np.asarray(r[k]).astype(np.float32) for r in R] for k in R[0] if k.startswith("dbg_")}
    return outs, dbg


def kernel(**inputs):
    cfg = dict(NPS=2, SEQ=2048, NSS=4, DEPTH=2)
    outs, _ = run(cfg, inputs, 8)
    return outs
```
